# Optimizing a Trainium2 kernel written in Bass

```python
import jax, jax.numpy as jnp
from jax import lax
import numpy as np

D_MODEL = 1024
BATCH = 8
SEQ = 4096
DEPTH = 1

CTX_LEN = 256
GRID_W = 64
EPS = 1e-6
N_MOD = 6

A_WIDTH = D_MODEL
A_GROUPS = 8
A_GROUP_DIM = A_WIDTH // A_GROUPS
A_CHUNK = 128

B_HEADS = 4
B_DK = D_MODEL // 2 // B_HEADS
B_DV = D_MODEL // B_HEADS
B_QK_WIDTH = B_HEADS * B_DK
B_V_WIDTH = B_HEADS * B_DV
GATE_RANK = 16
GATE_TEMP = 16.0
GLA_CHUNK = 64

D_FF = ((8 * D_MODEL // 3 + 255) // 256) * 256

IN_SIZES = (A_WIDTH, A_WIDTH, B_QK_WIDTH, B_QK_WIDTH, B_V_WIDTH, B_V_WIDTH,
            GATE_RANK, GATE_RANK, D_MODEL, D_MODEL)

kernel_name = "hybrid_gmlp_gla_diffusion_block"


def _split_offsets():
    offs, acc = [], 0
    for s in IN_SIZES[:-1]:
        acc += s
        offs.append(acc)
    return offs


def _rmsnorm(x, g):
    xf = x.astype(jnp.float32)
    y = xf * lax.rsqrt(jnp.mean(xf * xf, axis=-1, keepdims=True) + EPS)
    return (y * g.astype(jnp.float32)).astype(x.dtype)


def _layernorm(x, g, b):
    xf = x.astype(jnp.float32)
    mu = jnp.mean(xf, axis=-1, keepdims=True)
    xc = xf - mu
    y = xc * lax.rsqrt(jnp.mean(xc * xc, axis=-1, keepdims=True) + EPS)
    return (y * g.astype(jnp.float32) + b.astype(jnp.float32)).astype(x.dtype)


def _modulate(xn, shift, scale):
    return xn * (1 + scale) + shift


def _chunk_mlp(u, v, n_chunks, ln_v_g, ln_v_b, w_spatial, b_spatial):
    B, T, _ = u.shape
    u = jax.nn.gelu(u, approximate=False)
    v = _layernorm(jax.nn.gelu(v, approximate=False), ln_v_g, ln_v_b)
    vc = v.reshape(B, n_chunks, A_CHUNK, A_GROUPS, A_GROUP_DIM)
    mixed = jnp.einsum('gpq,bnqgc->bnpgc', w_spatial, vc) + jnp.transpose(b_spatial)[:, :, None]
    return u * mixed.reshape(B, T, A_WIDTH)


def _gla_heads(q, k, v, af, ab, w_af, b_af, w_ab, b_ab):
    B, T, _ = q.shape
    f32 = jnp.float32
    q = q.astype(f32).reshape(B, T, B_HEADS, B_DK) * (B_DK ** -0.5)
    k = k.astype(f32).reshape(B, T, B_HEADS, B_DK)
    v = v.astype(f32).reshape(B, T, B_HEADS, B_DV)
    lf = (jax.nn.log_sigmoid((af @ w_af + b_af).astype(f32)) / GATE_TEMP).reshape(B, T, B_HEADS, B_DK)
    lb = (jax.nn.log_sigmoid((ab @ w_ab + b_ab).astype(f32)) / GATE_TEMP).reshape(B, T, B_HEADS, B_DK)
    return q, k, v, lf, lb


def _gla_scan(q, k, v, log_a, s0):
    B, T, H, DK = q.shape
    DV = v.shape[-1]
    n = T // GLA_CHUNK

    def to_chunks(t):
        return jnp.moveaxis(t.reshape(B, n, GLA_CHUNK, H, t.shape[-1]), 1, 0)

    mask = jnp.tril(jnp.ones((GLA_CHUNK, GLA_CHUNK), dtype=bool))

    def step(s, xs):
        qc, kc, vc, gc = xs
        b = jnp.cumsum(gc, axis=1)
        b_last = b[:, -1]
        q_dec = qc * jnp.exp(b)
        k_inv = kc * jnp.exp(-b)
        k_state = kc * jnp.exp(b_last[:, None] - b)
        o_inter = jnp.einsum('bchk,bhkv->bchv', q_dec, s)
        att = jnp.where(mask, jnp.einsum('bihk,bjhk->bhij', q_dec, k_inv), 0.0)
        o_intra = jnp.einsum('bhij,bjhv->bihv', att, vc)
        s_new = jnp.exp(b_last)[..., None] * s + jnp.einsum('bchk,bchv->bhkv', k_state, vc)
        return s_new, o_inter + o_intra

    s_fin, o = lax.scan(step, s0, (to_chunks(q), to_chunks(k), to_chunks(v), to_chunks(log_a)))
    o = jnp.moveaxis(o, 0, 1).reshape(B, T, H, DV)
    return o, s_fin


def _gla_out(o, r, gla_norm_g, dtype):
    B, T = o.shape[0], o.shape[1]
    o = o * lax.rsqrt(jnp.mean(o * o, axis=-1, keepdims=True) + EPS)
    o = o.reshape(B, T, B_V_WIDTH) * gla_norm_g.astype(jnp.float32)
    return o.astype(dtype) * jax.nn.silu(r)


def _token_mixer(h_lat, h_ctx, w_in, ln_v_g, ln_v_b, w_spatial, b_spatial,
                 w_alpha_f, b_alpha_f, w_alpha_b, b_alpha_b, gla_norm_g,
                 w_branch_a, w_branch_b, w_out, with_ctx_out):
    offs = _split_offsets()
    u_l, v_l, q_l, k_l, vv_l, r_l, af_l, ab_l, ga_l, gb_l = jnp.split(h_lat @ w_in, offs, axis=-1)
    u_c, v_c, q_c, k_c, vv_c, r_c, af_c, ab_c, ga_c, gb_c = jnp.split(h_ctx @ w_in, offs, axis=-1)
    B = h_lat.shape[0]

    ql, kl, vl, lfl, lbl = _gla_heads(q_l, k_l, vv_l, af_l, ab_l, w_alpha_f, b_alpha_f, w_alpha_b, b_alpha_b)
    qc, kc, vc, lfc, lbc = _gla_heads(q_c, k_c, vv_c, af_c, ab_c, w_alpha_f, b_alpha_f, w_alpha_b, b_alpha_b)
    s0 = jnp.zeros((B, B_HEADS, B_DK, B_DV), jnp.float32)
    fl = lambda t: jnp.flip(t, axis=1)
    o_cf, s_cf = _gla_scan(qc, kc, vc, lfc, s0)
    o_lf, _ = _gla_scan(ql, kl, vl, lfl, s_cf)
    o_cb, s_cb = _gla_scan(fl(qc), fl(kc), fl(vc), fl(lbc), s0)
    o_lb, _ = _gla_scan(fl(ql), fl(kl), fl(vl), fl(lbl), s_cb)
    b_lat = _gla_out(o_lf + fl(o_lb), r_l, gla_norm_g, h_lat.dtype)

    rows = h_lat.shape[1] // GRID_W
    a_lat = _chunk_mlp(u_l, v_l, rows // (A_CHUNK // GRID_W), ln_v_g, ln_v_b, w_spatial, b_spatial)

    def merge(a_out, b_out, ga, gb):
        y = jax.nn.sigmoid(ga) * (a_out @ w_branch_a) + jax.nn.sigmoid(gb) * (b_out @ w_branch_b)
        return y @ w_out

    out_lat = merge(a_lat, b_lat, ga_l, gb_l)
    if not with_ctx_out:
        return out_lat, None
    b_ctx = _gla_out(o_cf + fl(o_cb), r_c, gla_norm_g, h_ctx.dtype)
    a_ctx = _chunk_mlp(u_c, v_c, h_ctx.shape[1] // A_CHUNK, ln_v_g, ln_v_b, w_spatial, b_spatial)
    return out_lat, merge(a_ctx, b_ctx, ga_c, gb_c)


def _swiglu(h, w_ffn_in, w_ffn_out):
    a, g = jnp.split(h @ w_ffn_in, 2, axis=-1)
    return (jax.nn.silu(g) * a) @ w_ffn_out


def setup_inputs(seed: int = 0) -> dict:
    key = jax.random.key(seed)
    ks = jax.random.split(key, 24)
    f32 = jnp.float32
    nrm = lambda k, shape, s: jax.random.normal(k, shape, f32) * s
    L, D = DEPTH, D_MODEL
    return {
        "x": nrm(ks[0], (BATCH, SEQ, D), 1.0),
        "c": nrm(ks[1], (BATCH, D), 1.0),
        "ctx": nrm(ks[2], (BATCH, CTX_LEN, D), 1.0),
        "c_ctx": nrm(ks[3], (D,), 1.0),
        "w_ada": nrm(ks[4], (L, D, N_MOD * D), 0.5 * D ** -0.5),
        "b_ada": nrm(ks[5], (L, N_MOD * D), 0.02),
        "norm1_g": 1.0 + nrm(ks[6], (L, D), 0.02),
        "w_in": nrm(ks[7], (L, D, sum(IN_SIZES)), D ** -0.5),
        "ln_v_g": 1.0 + nrm(ks[8], (L, A_WIDTH), 0.02),
        "ln_v_b": nrm(ks[9], (L, A_WIDTH), 0.02),
        "w_spatial": nrm(ks[10], (L, A_GROUPS, A_CHUNK, A_CHUNK), A_CHUNK ** -0.5),
        "b_spatial": 1.0 + nrm(ks[11], (L, A_GROUPS, A_CHUNK), 0.02),
        "w_alpha_f": nrm(ks[12], (L, GATE_RANK, B_QK_WIDTH), GATE_RANK ** -0.5),
        "b_alpha_f": nrm(ks[13], (L, B_QK_WIDTH), 0.02),
        "w_alpha_b": nrm(ks[14], (L, GATE_RANK, B_QK_WIDTH), GATE_RANK ** -0.5),
        "b_alpha_b": nrm(ks[15], (L, B_QK_WIDTH), 0.02),
        "gla_norm_g": 1.0 + nrm(ks[16], (L, B_V_WIDTH), 0.02),
        "w_branch_a": nrm(ks[17], (L, A_WIDTH, D), A_WIDTH ** -0.5),
        "w_branch_b": nrm(ks[18], (L, B_V_WIDTH, D), B_V_WIDTH ** -0.5),
        "w_out": nrm(ks[19], (L, D, D), D ** -0.5),
        "norm2_g": 1.0 + nrm(ks[20], (L, D), 0.02),
        "w_ffn_in": nrm(ks[21], (L, D, 2 * D_FF), D ** -0.5),
        "w_ffn_out": nrm(ks[22], (L, D_FF, D), D_FF ** -0.5),
        "final_norm_g": 1.0 + nrm(ks[23], (D,), 0.02),
    }


def reference(x, c, ctx, c_ctx, w_ada, b_ada, norm1_g, w_in, ln_v_g, ln_v_b, w_spatial, b_spatial,
              w_alpha_f, b_alpha_f, w_alpha_b, b_alpha_b, gla_norm_g, w_branch_a, w_branch_b,
              w_out, norm2_g, w_ffn_in, w_ffn_out, final_norm_g):
    for l in range(DEPTH):
        update_ctx = l < DEPTH - 1
        mod = jnp.split((jax.nn.silu(c) @ w_ada[l] + b_ada[l])[:, None, :], N_MOD, axis=-1)
        mod_c = jnp.split(jax.nn.silu(c_ctx) @ w_ada[l] + b_ada[l], N_MOD, axis=-1)
        sh1, sc1, g1, sh2, sc2, g2 = mod
        sh1c, sc1c, g1c, sh2c, sc2c, g2c = mod_c

        h_lat = _modulate(_rmsnorm(x, norm1_g[l]), sh1, sc1)
        h_ctx = _modulate(_rmsnorm(ctx, norm1_g[l]), sh1c, sc1c)
        mix_l, mix_c = _token_mixer(h_lat, h_ctx, w_in[l], ln_v_g[l], ln_v_b[l], w_spatial[l], b_spatial[l],
                                    w_alpha_f[l], b_alpha_f[l], w_alpha_b[l], b_alpha_b[l], gla_norm_g[l],
                                    w_branch_a[l], w_branch_b[l], w_out[l], update_ctx)
        x = x + g1 * mix_l
        x = x + g2 * _swiglu(_modulate(_rmsnorm(x, norm2_g[l]), sh2, sc2), w_ffn_in[l], w_ffn_out[l])
        if update_ctx:
            ctx = ctx + g1c * mix_c
            ctx = ctx + g2c * _swiglu(_modulate(_rmsnorm(ctx, norm2_g[l]), sh2c, sc2c), w_ffn_in[l], w_ffn_out[l])
    return _rmsnorm(x, final_norm_g)
```

```python
import contextlib
import numpy as np
import concourse.bass as bass
import concourse.mybir as mybir
from concourse.bass_utils import run_bass_kernel_spmd

F32 = mybir.dt.float32
BF16 = mybir.dt.bfloat16
AF = mybir.ActivationFunctionType
ALU = mybir.AluOpType

N_DMA_SEMS = 32
T_LAT = 4096
T_CTX = 256
D = 1024
DFF = 2816
NSLOT = 37
RING = 4


class Buf:
    __slots__ = ("name", "last_w", "readers")

    def __init__(self, name=""):
        self.name = name
        self.last_w = None
        self.readers = []


class Op:
    __slots__ = ("eng", "fn", "deps", "pos", "dma", "dsem", "dval", "signal", "vc", "val", "gid")

    def __init__(self, eng, fn, dma):
        self.eng = eng
        self.fn = fn
        self.deps = {}
        self.dma = dma
        self.signal = False
        self.vc = None


class Prog:
    ENGS = ("pe", "act", "dve", "pool", "sp")

    def __init__(self, nc):
        self.nc = nc
        self.ops = []
        self.q = {e: [] for e in self.ENGS}
        self.dma_by_q = {}
        self.dma_ops = []
        self._cap = None

    def capture(self):
        self._cap = []

    def end_capture(self):
        c = self._cap
        self._cap = None
        return c

    def commit(self, *streams):
        items = []
        for si, st in enumerate(streams):
            n = len(st)
            units = []
            for it in st:
                if units and it[0] == "pe" and units[-1][-1][0] == "pe" and it[3] == units[-1][-1][3]:
                    units[-1].append(it)
                else:
                    units.append([it])
            pos = 0
            for ui, u in enumerate(units):
                items.append(((pos + 0.5 * len(u)) / n, si, ui, u))
                pos += len(u)
        items.sort(key=lambda t: (t[0], t[1], t[2]))
        for _, _, _, u in items:
            for it in u:
                self.add(*it)

    def add(self, eng, fn, reads=(), writes=(), dma=False):
        if self._cap is not None:
            self._cap.append((eng, fn, tuple(reads), tuple(writes), dma))
            return None
        op = Op(eng, fn, dma)
        op.gid = len(self.ops)
        for b in reads:
            if b.last_w is not None:
                op.deps[b.last_w] = True
            b.readers.append(op)
        for b in writes:
            if b.last_w is not None and b.last_w not in op.deps:
                op.deps[b.last_w] = False
            for r in b.readers:
                if r is not op and r not in op.deps:
                    op.deps[r] = False
            b.last_w = op
            b.readers = []
        if dma:
            lst = self.dma_by_q.setdefault(eng, [])
            i = len(lst)
            half = N_DMA_SEMS // 2
            base = 0 if eng == "sp" else half
            op.dsem = base + i % half
            op.dval = 16 * (i // half + 1)
            if i >= half:
                op.deps[lst[i - half]] = True
            lst.append(op)
            self.dma_ops.append(op)
        op.pos = len(self.q[eng])
        self.q[eng].append(op)
        self.ops.append(op)
        return op

    def emit(self):
        nc = self.nc
        know = {e: {} for e in self.ENGS}
        waits = {}
        for op in self.ops:
            K = know[op.eng]
            wl = []
            best = {}
            for d, raw in op.deps.items():
                if d.dma:
                    key = ("d", d.dsem)
                    need = d.dval
                else:
                    if d.eng == op.eng and not op.dma:
                        if op.eng == "pe":
                            continue
                    key = d.eng
                    need = d.pos
                if key not in best or best[key][0] < need:
                    best[key] = (need, d)
            for key, (need, d) in best.items():
                if K.get(key, -1) >= need:
                    continue
                wl.append(d)
                d.signal = True
                for k2, v2 in d.vc.items():
                    if K.get(k2, -1) < v2:
                        K[k2] = v2
            waits[op.gid] = wl
            vc = dict(K)
            if op.dma:
                vc[("d", op.dsem)] = op.dval
            else:
                vc[op.eng] = op.pos
            op.vc = vc
        for e in self.ENGS:
            c = 0
            for op in self.q[e]:
                if op.dma:
                    continue
                if op.signal:
                    c += 1
                    op.val = c
        self.n_waits = sum(len(w) for w in waits.values())
        with contextlib.ExitStack() as st:
            csem = {e: st.enter_context(nc.semaphore("cs_" + e)) for e in self.ENGS}
            dsem = [st.enter_context(nc.semaphore("ds%d" % i)) for i in range(N_DMA_SEMS)]
            block = st.enter_context(nc.Block())

            def run(ename):
                def body(eng):
                    for op in self.q[ename]:
                        for d in waits[op.gid]:
                            if d.dma:
                                eng.wait_ge(dsem[d.dsem], d.dval)
                            else:
                                eng.wait_ge(csem[d.eng], d.val)
                        ins = op.fn(eng)
                        if op.dma:
                            ins.then_inc(dsem[op.dsem], 16)
                        elif op.signal:
                            ins.then_inc(csem[op.eng], 1)
                    if ename == "sp":
                        last = {}
                        for d in self.dma_ops:
                            last[d.dsem] = max(last.get(d.dsem, 0), d.dval)
                        for s, v in last.items():
                            eng.wait_ge(dsem[s], v)

                return body

            block.tensor(run("pe"))
            block.scalar(run("act"))
            block.vector(run("dve"))
            block.gpsimd(run("pool"))
            block.sync(run("sp"))


def build_nc(dbg=None, stop_after=None):
    nc = bass.Bass("TRN2", target_bir_lowering=False)
    P = Prog(nc)

    def din(name, shape):
        return nc.dram_tensor(name, list(shape), F32, kind="ExternalInput").ap()

    x = din("x", [T_LAT, D])
    ctx = din("ctx", [T_CTX, D])
    c2 = din("c2", [D, 2])
    w_ada = din("w_ada", [D, 6 * D])
    b_ada = din("b_ada", [1, 6 * D])
    vecs_fm = din("vecs_fm", [128, 3, 8])
    rows = din("rows", [4, D])
    w_in = din("w_in", [D, 7200])
    w_spatial = din("w_spatial", [8, 128, 128])
    walpha = din("walpha", [2, 17, 512])
    w_ba = din("w_branch_a", [D, D])
    w_bb = din("w_branch_b", [D, D])
    w_out = din("w_out", [D, D])
    w_fi = din("w_ffn_in", [D, 2 * DFF])
    w_fo = din("w_ffn_out", [DFF, D])
    out = nc.dram_tensor("out", [T_LAT, D], F32, kind="ExternalOutput").ap()
    wsc = nc.dram_tensor("wsc", [NSLOT, 128, 4096], BF16).ap()
    obs = nc.dram_tensor("obs", [T_LAT, D], F32).ap()
    prj = nc.dram_tensor("prj", [8, 128, 10240], BF16).ap()
    dbg_out = {}
    if dbg:
        for k, shp in dbg.items():
            dbg_out[k] = nc.dram_tensor("dbg_" + k, list(shp), F32, kind="ExternalOutput").ap()

    def sb(name, shape, dt=F32):
        return nc.alloc_sbuf_tensor(name, list(shape), dt)

    IDF = sb("IDF", [128, 128])
    ID = sb("ID", [128, 128], BF16)
    MASK = {d: sb("MASK" + d, [128, 128], BF16) for d in "fb"}
    TRIb = {d: sb("TRIb" + d, [128, 128], BF16) for d in "fb"}
    TRISb = {d: sb("TRISb" + d, [128, 128], BF16) for d in "fb"}
    GHL = sb("GHL", [128, 4, 2, 512], BF16)
    ghl_ctr = [0]
    NHALF = sb("NHALF", [128, 8])
    ONESR = sb("ONESR", [1, 128], BF16)
    SELB = sb("SELB", [2, 128])
    SELC = sb("SELC", [2, 2])
    C2 = sb("C2", [128, 8, 2])
    SC = sb("SC", [128, 8, 2], BF16)
    VFM = sb("VFM", [128, 3, 8])
    MODF = sb("MODF", [128, 6, 8])
    S1 = sb("S1", [128, 8]); S1C = sb("S1C", [128, 8]); S2 = sb("S2", [128, 8])
    G1H = sb("G1H", [128, D]); G2B = sb("G2B", [128, D]); FNG = sb("FNG", [128, D])
    LNG = sb("LNG", [128, D]); LNB = sb("LNB", [128, D])
    BSROW = sb("BSROW", [1, D], BF16)
    WSPT = sb("WSPT", [128, 8, 128], BF16)
    WAL = {d: sb("WAL" + d, [33, 512], BF16) for d in "fb"}
    WAF = sb("WAF", [128, 8, 32], BF16)
    S32 = sb("S32", [128, 4, 256])
    SBF2 = sb("SBF2", [128, 2, 4, 256], BF16)
    spar = [0]
    WS = [sb("WS%d" % i, [128, 8, 512], BF16) for i in range(RING)]
    X = sb("X", [128, 4, D])
    XN = sb("XN", [128, 4, D], BF16)
    HT = sb("HT", [128, 8, 512], BF16)
    AFAB = sb("AFAB", [33, 512], BF16)
    STAT = sb("STAT", [128, 16])
    BNS = sb("BNS", [128, 2, 2, 6]); BNA = sb("BNA", [128, 2, 2])
    GATE = sb("GATE", [128, 4 * 4 * 512])
    MOD = GATE[0:2, 0:6 * D]
    G_ = GATE[:, 0:2048].rearrange("p (t c) -> p t c", c=512)
    E_ = GATE[:, 2048:4096].rearrange("p (h c) -> p h c", c=512)
    EI_ = GATE[:, 4096:6144].rearrange("p (h c) -> p h c", c=512)
    ER_ = GATE[:, 6144:8192].rearrange("p (t c) -> p t c", c=512)
    ACTT = GATE[:, 0:22 * 256].bitcast(BF16).rearrange("p (k c) -> p k c", c=512)
    ELAST2 = sb("ELAST", [128, 2, 4, 4])
    elpar = [0]
    GLA = sb("GLA", [128, 10240], BF16)
    QD = GLA[:, 0:2048].rearrange("p (h c) -> p h c", c=512)
    KI = GLA[:, 2048:4096].rearrange("p (h c) -> p h c", c=512)
    KST = GLA[:, 4096:6144].rearrange("p (t c) -> p t c", c=512)
    VV = GLA[:, 6144:10240].rearrange("p (t c) -> p t c", c=1024)
    TA = GLA[:, 0:4096].rearrange("p (k c) -> p k c", c=512)
    TB = GLA[:, 4096:8192].rearrange("p (k c) -> p k c", c=512)
    ATT = sb("ATT", [128, 4, 128], BF16)
    O = sb("O", [128, D]); OB = sb("OB", [128, D])
    BADA = O[0:2, :].rearrange("p (a b) -> p a b", b=512)
    TSCR = OB[:, 0:128]
    UT = sb("UT", [128, 8, 512], BF16)
    VN = sb("VN", [128, 4, D], BF16)
    YT = VN[:].rearrange("p t c -> p (t c)").rearrange("p (k c) -> p k c", c=512)
    GV2 = sb("GV2", [128, 2, D])
    GV = GV2[:, 0, :]
    WSPF = GV2[:, 0, :].rearrange("p (g q) -> p g q", q=128)
    RT = sb("RT", [128, 8, 512], BF16)
    TMP = sb("TMP", [128, 512])
    NPJ = 4
    PJ = [nc.alloc_psum_tensor("PJ%d" % i, [128, 512], F32) for i in range(NPJ)]
    PO = [nc.alloc_psum_tensor("PO%d" % i, [128, 512], F32) for i in range(4)]
    bPJ = [Buf("PJ%d" % i) for i in range(NPJ)]
    bPO = [Buf("PO%d" % i) for i in range(4)]
    pj_pool = [[0, 1, 2, 3]]
    pj_ctr = {}
    ws_pool = [[0, 1, 2, 3]]

    def next_pj():
        pool = tuple(pj_pool[0])
        c = pj_ctr.get(pool, 0)
        pj_ctr[pool] = c + 1
        i = pool[c % len(pool)]
        return PJ[i], bPJ[i]

    def next_ptr():
        pj, b = next_pj()
        return pj[:].bitcast(BF16)[:, 0:512], b

    B = {}

    def bf(name):
        if name not in B:
            B[name] = Buf(name)
        return B[name]

    bWS = [Buf("WS%d" % i) for i in range(RING)]
    bWSC = [Buf("WSC%d" % i) for i in range(NSLOT)]
    bOBS = [Buf("OBS%d" % i) for i in range(32)]
    bPRJ = [[Buf("PRJ%d_%d" % (i, j)) for j in range(4)] for i in range(8)]
    ring_ctr = {}

    def next_ring():
        pool = tuple(ws_pool[0])
        c = ring_ctr.get(pool, 0)
        ring_ctr[pool] = c + 1
        return pool[c % len(pool)]

    def tap(name, ap_sb, reads, rows_=128):
        if dbg and name in dbg_out:
            P.add("pool", lambda e: e.dma_start(out=dbg_out[name], in_=ap_sb), reads=reads, dma=True)

    cst = bf("const")

    def tri_make(dst, base, cm, step, val):
        P.add("pool", lambda e: e.memset(TSCR, val), writes=[bf("OB")])
        P.add("pool", lambda e: e.affine_select(out=TSCR, in_=TSCR, pattern=[[step, 128]], compare_op=ALU.is_ge,
                                                fill=0.0, base=base, channel_multiplier=cm), reads=[bf("OB")], writes=[bf("OB")])
        P.add("pool", lambda e: e.tensor_copy(out=dst[:], in_=TSCR), reads=[bf("OB")], writes=[cst])

    NEG = -1.0 / 16.0
    P.add("pool", lambda e: e.memset(IDF[:], 1.0), writes=[cst])
    P.add("pool", lambda e: e.affine_select(out=IDF[:], in_=IDF[:], pattern=[[-1, 128]], compare_op=ALU.is_equal,
                                            fill=0.0, base=0, channel_multiplier=1), reads=[cst], writes=[cst])
    P.add("pool", lambda e: e.tensor_copy(out=ID[:], in_=IDF[:]), reads=[cst], writes=[cst])
    tri_make(TRIb["f"], 0, -1, 1, NEG)
    tri_make(TRIb["b"], 0, 1, -1, NEG)
    tri_make(TRISb["f"], -1, 1, -1, NEG)
    tri_make(TRISb["b"], -1, -1, 1, NEG)
    tri_make(MASK["f"], 0, -1, 1, 1.0)
    tri_make(MASK["b"], 0, 1, -1, 1.0)
    P.add("pool", lambda e: e.memset(NHALF[:], -0.5), writes=[cst])
    P.add("pool", lambda e: e.memset(ONESR[:], 1.0), writes=[cst])
    P.add("pool", lambda e: e.memset(SELB[:], 1.0), writes=[cst])
    P.add("pool", lambda e: e.affine_select(out=SELB[:], in_=SELB[:], pattern=[[0, 128]], compare_op=ALU.is_equal,
                                            fill=0.0, base=0, channel_multiplier=1), reads=[cst], writes=[cst])
    P.add("pool", lambda e: e.tensor_copy(out=SELC[:], in_=IDF[0:2, 0:2]), reads=[cst], writes=[cst])
    P.add("pool", lambda e: e.memset(AFAB[:], 1.0), writes=[bf("AFAB")])
    for d in "fb":
        P.add("pool", lambda e, d=d: e.memset(WAL[d][:], 0.0), writes=[bf("WAL")])
    P.add("sp", lambda e: e.dma_start(out=C2[:], in_=c2.rearrange("(kc p) j -> p kc j", p=128)), writes=[bf("C2")], dma=True)
    P.add("sp", lambda e: e.dma_start(out=VFM[:], in_=vecs_fm), writes=[bf("VFM")], dma=True)
    P.add("sp", lambda e: e.dma_start(out=FNG[:], in_=rows[0:1, :].partition_broadcast(128)), writes=[cst], dma=True)
    P.add("sp", lambda e: e.dma_start(out=LNG[:], in_=rows[1:2, :].partition_broadcast(128)), writes=[cst], dma=True)
    P.add("sp", lambda e: e.dma_start(out=LNB[:], in_=rows[2:3, :].partition_broadcast(128)), writes=[cst], dma=True)
    P.add("sp", lambda e: e.dma_start(out=WSPF, in_=w_spatial.rearrange("g p q -> p g q")), writes=[bf("GV0_0"), bf("GV0_1")], dma=True)
    P.add("pool", lambda e: e.dma_start(out=BSROW[:], in_=rows[3:4, :]), writes=[cst], dma=True)
    P.add("pool", lambda e: e.dma_start(out=WAL["f"][0:16, :], in_=walpha[0, 0:16, :]), writes=[bf("WAL")], dma=True)
    P.add("pool", lambda e: e.dma_start(out=WAL["f"][32:33, :], in_=walpha[0, 16:17, :]), writes=[bf("WAL")], dma=True)
    P.add("pool", lambda e: e.dma_start(out=WAL["b"][16:32, :], in_=walpha[1, 0:16, :]), writes=[bf("WAL")], dma=True)
    P.add("pool", lambda e: e.dma_start(out=WAL["b"][32:33, :], in_=walpha[1, 16:17, :]), writes=[bf("WAL")], dma=True)
    P.add("pool", lambda e: e.dma_start(out=WAF[:], in_=w_in[:, 5120:5152].rearrange("(kc p) c -> p kc c", p=128)),
          writes=[bf("WAF")], dma=True)

    def cast_k1024(s, W, col0):
        src = W[:, col0:col0 + 512].rearrange("(kc p) c -> p kc c", p=128)
        dst = wsc[s].rearrange("p (kc c) -> p kc c", c=512)
        P.add("pool", lambda e: e.dma_start(out=dst, in_=src), writes=[bWSC[s]], dma=True)

    def cast_ffi(s, j):
        dst = wsc[s].rearrange("p (kc c) -> p kc c", c=512)
        for part, c0 in ((0, j * 256), (1, DFF + j * 256)):
            src = w_fi[:, c0:c0 + 256].rearrange("(kc p) c -> p kc c", p=128)
            d2 = dst[:, :, part * 256:(part + 1) * 256]
            P.add("pool", lambda e, src=src, d2=d2: e.dma_start(out=d2, in_=src), writes=[bWSC[s]], dma=True)

    def cast_ffo(s, h, pc):
        kc0 = 8 * pc
        nk = 8 if pc < 2 else 6
        src = w_fo[kc0 * 128:(kc0 + nk) * 128, h * 512:(h + 1) * 512].rearrange("(kc p) c -> p kc c", p=128)
        dst = wsc[s][:, 0:nk * 512].rearrange("p (kc c) -> p kc c", c=512)
        P.add("pool", lambda e: e.dma_start(out=dst, in_=src), writes=[bWSC[s]], dma=True)

    SL = dict(u=(0, 1), v=(2, 3), q=(4,), k=(5,), vv=(6, 7), r=(8, 9), ga=(10, 11), gb=(12, 13),
              ba=(14, 15), bb=(16, 17), wo=(18, 19))
    COL = dict(u=0, v=1024, q=2048, k=2560, vv=3072, r=4096, ga=5152, gb=6176)

    def cast_group(names):
        for n in names:
            for i, s in enumerate(SL[n]):
                if n in COL:
                    cast_k1024(s, w_in, COL[n] + 512 * i)
                else:
                    cast_k1024(s, dict(ba=w_ba, bb=w_bb, wo=w_out)[n], 512 * i)

    cast_group(["k", "vv", "q"])

    P.add("act", lambda e: e.activation(out=SC[:], in_=C2[:], func=AF.Silu), reads=[bf("C2")], writes=[bf("SC")])
    bada_b = []
    MODB = [bf("MOD")] + [bf("%s%d" % (p_, i_)) for p_ in ("G", "E", "EI") for i_ in range(4)]
    for i in range(12):
        r = next_ring()
        src = w_ada[:, i * 512:(i + 1) * 512].rearrange("(kc p) c -> p kc c", p=128)
        P.add("pool", lambda e, src=src, r=r: e.dma_start(out=WS[r][:], in_=src), writes=[bWS[r]], dma=True)
        bb_ = bf("O")
        P.add("sp", lambda e, i=i: e.dma_start(out=BADA[:, i % 2, :], in_=b_ada[0:1, i * 512:(i + 1) * 512].partition_broadcast(2)),
              writes=[bb_], dma=True)
        pj, bpj = next_pj()
        for kc in range(8):
            P.add("pe", lambda e, kc=kc, r=r, pj=pj: e.matmul(pj[0:2, :], lhsT=SC[:, kc, :], rhs=WS[r][:, kc, :],
                                                            start=(kc == 0), stop=(kc == 7)),
                  reads=[bf("SC"), bWS[r]], writes=[bpj])
        P.add("dve", lambda e, i=i, pj=pj: e.tensor_tensor(out=MOD[:, i * 512:(i + 1) * 512], in0=pj[0:2, :],
                                                         in1=BADA[:, i % 2, :], op=ALU.add),
              reads=[bpj, bb_], writes=MODB)
    pending_casts = []
    for n in ["r", "u", "v", "ga", "gb", "ba", "bb", "wo"]:
        pending_casts.append(lambda n=n: cast_group([n]))
    for j in range(11):
        pending_casts.append(lambda j=j: cast_ffi(20 + j, j))
    for h in range(2):
        for pc in range(3):
            pending_casts.append(lambda h=h, pc=pc: cast_ffo(31 + h * 3 + pc, h, pc))

    def issue_casts(n):
        for _ in range(n):
            if pending_casts:
                pending_casts.pop(0)()
    tap("mod", MOD, MODB)
    pj, bpj = next_pj()
    spec = [(0, 0), (1, 0), (0, 1), (1, 1), (3, 0), (4, 0)]
    for vi, (blk, row) in enumerate(spec):
        for kc in range(8):
            c0 = blk * D + kc * 128
            P.add("pe", lambda e, vi=vi, kc=kc, c0=c0, row=row, pj=pj: e.matmul(
                pj[:, vi * 8 + kc:vi * 8 + kc + 1], lhsT=MOD[0:2, c0:c0 + 128], rhs=SELC[0:2, row:row + 1],
                start=True, stop=True), reads=MODB + [cst], writes=[bpj])
    P.add("dve", lambda e, pj=pj: e.tensor_copy(out=MODF[:].rearrange("p a b -> p (a b)"), in_=pj[:, 0:48]),
          reads=[bpj], writes=[bf("MODF")])
    for (dst, sci, gi) in ((S1, 1, 0), (S1C, 3, 0), (S2, 5, 1)):
        P.add("dve", lambda e, dst=dst, sci=sci, gi=gi: e.scalar_tensor_tensor(
            out=dst[:], in0=MODF[:, sci, :], scalar=1.0, in1=VFM[:, gi, :], op0=ALU.add, op1=ALU.mult),
            reads=[bf("MODF"), bf("VFM")], writes=[bf("S")])
    SH1 = MODF[:, 0, :]; SH1C = MODF[:, 2, :]; SH2 = MODF[:, 4, :]
    for (dst, blk, scl) in ((G1H, 2, 0.5), (G2B, 5, 1.0)):
        for hf in range(2):
            pj, bpj = next_pj()
            c0 = blk * D + hf * 512
            P.add("pe", lambda e, c0=c0, pj=pj: e.matmul(pj[:], lhsT=SELB[0:2, :], rhs=MOD[0:2, c0:c0 + 512],
                                                       start=True, stop=True), reads=MODB + [cst], writes=[bpj])
            P.add("act", lambda e, dst=dst, hf=hf, scl=scl, pj=pj: e.activation(
                out=dst[:, hf * 512:(hf + 1) * 512], in_=pj[:], func=AF.Copy, scale=scl), reads=[bpj], writes=[cst])
    P.add("dve", lambda e: e.tensor_copy(out=UT[:, :, 0:128], in_=WSPF), reads=[bf("GV0_0"), bf("GV0_1")], writes=[bf("UT%d" % i_) for i_ in range(8)])
    for g in range(8):
        pt, bpt = next_ptr()
        P.add("pe", lambda e, g=g, pt=pt: e.transpose(out=pt[:, 0:128], in_=UT[:, g, 0:128], identity=ID[:]),
              reads=[bf("UT%d" % g), cst], writes=[bpt])
        P.add("dve", lambda e, g=g, pt=pt: e.tensor_copy(out=WSPT[:, g, :], in_=pt[:, 0:128]), reads=[bpt], writes=[cst])

    def bl(prefix, n):
        return [bf("%s%d" % (prefix, i)) for i in range(n)]

    GALIAS = bl("G", 4) + bl("E", 4) + bl("EI", 4)

    def load_slot(s, ncols=4096):
        r = next_ring()
        P.add("sp", lambda e: e.dma_start(out=WS[r][:].rearrange("p a b -> p (a b)")[:, 0:ncols], in_=wsc[s][:, 0:ncols]),
              reads=[bWSC[s]], writes=[bWS[r]], dma=True)
        return WS[r], bWS[r]

    def front(src_ap, ntile, Sv, SHv, tagsrc):
        for t in range(ntile):
            P.add("sp", lambda e, t=t: e.dma_start(out=X[:, t, :], in_=src_ap[t * 128:(t + 1) * 128, :]),
                  writes=[bf("X%d" % t)], dma=True)
        norm_to_HT(ntile, Sv, SHv)

    def norm_to_HT(ntile, Sv, SHv):
        for t in range(ntile):
            P.add("act", lambda e, t=t: e.activation(out=XN[:, t, :], in_=X[:, t, :], func=AF.Square, accum_out=STAT[:, t:t + 1]),
                  reads=[bf("X%d" % t)], writes=[bf("STa%d" % t), bf("XN%d" % t)])
        P.add("dve", lambda e: e.tensor_scalar(out=STAT[:, 4:4 + ntile], in0=STAT[:, 0:ntile], scalar1=1.0 / D, scalar2=1e-6,
                                               op0=ALU.mult, op1=ALU.add), reads=bl("STa", ntile), writes=[bf("STb")] + bl("STb", 4))
        P.add("pool", lambda e: e.tensor_tensor(out=STAT[:, 8:8 + ntile], in0=STAT[:, 4:4 + ntile], in1=NHALF[:, 0:ntile], op=ALU.pow),
              reads=[bf("STb"), cst], writes=[bf("STc")] + bl("STc", 4))
        for t in range(ntile):
            P.add("dve", lambda e, t=t: e.tensor_scalar(out=XN[:, t, :], in0=X[:, t, :], scalar1=STAT[:, 8 + t:9 + t], scalar2=None,
                                                        op0=ALU.mult), reads=[bf("X%d" % t), bf("STc")], writes=[bf("XN%d" % t)])
        xn_to_fm(ntile, lambda e, kc, pt, n: e.tensor_scalar(out=HT[:, kc, 0:n], in0=pt[:, 0:n], scalar1=Sv[:, kc:kc + 1],
                                                             scalar2=SHv[:, kc:kc + 1], op0=ALU.mult, op1=ALU.add),
                 lambda kc: [bf("S"), bf("MODF")], lambda kc: [bf("HT%d" % kc)],
                 act_evac=lambda e, kc, pt, n: e.activation(out=HT[:, kc, 0:n], in_=pt[:, 0:n], func=AF.Identity,
                                                            scale=Sv[:, kc:kc + 1], bias=SHv[:, kc:kc + 1]))

    def xn_to_fm(ntile, evac, ereads, ewrites, act_evac=None):
        n = ntile * 128
        for kc in range(8):
            pt, bpt = next_ptr()
            for t in range(ntile):
                P.add("pe", lambda e, t=t, kc=kc, pt=pt: e.transpose(out=pt[:, t * 128:(t + 1) * 128], in_=XN[:, t, kc * 128:(kc + 1) * 128],
                                                                     identity=ID[:]), reads=[bf("XN%d" % t), cst], writes=[bpt])
            if act_evac is not None and kc % 2 == 1:
                P.add("act", lambda e, kc=kc, pt=pt: act_evac(e, kc, pt, n), reads=[bpt] + ereads(kc), writes=ewrites(kc))
            else:
                P.add("dve", lambda e, kc=kc, pt=pt: evac(e, kc, pt, n), reads=[bpt] + ereads(kc), writes=ewrites(kc))

    def proj_fm(ws, bws, c, n, evac_eng, evac, ereads, ewrites, M=128, lhs_cols=None, src=None, bsrc="HT"):
        src = HT if src is None else src
        pj, bpj = next_pj()
        for kc in range(8):
            lhs = ws[:, kc, c * 128:c * 128 + M] if lhs_cols is None else ws[:, kc, lhs_cols[0]:lhs_cols[1]]
            P.add("pe", lambda e, kc=kc, lhs=lhs, pj=pj: e.matmul(pj[0:M, 0:n], lhsT=lhs, rhs=src[:, kc, 0:n],
                                                                start=(kc == 0), stop=(kc == 7)),
                  reads=[bws, bf("%s%d" % (bsrc, kc))], writes=[bpj])
        P.add(evac_eng, lambda e, pj=pj: evac(e, pj), reads=[bpj] + ereads, writes=ewrites)

    def proj_tm(ws, bws, t, evac_eng, evac, ereads, ewrites, src=None, bsrc=None):
        src = HT if src is None else src
        bsrc = (lambda kc: bf("HT%d" % kc)) if bsrc is None else bsrc
        pj, bpj = next_pj()
        for kc in range(8):
            P.add("pe", lambda e, kc=kc, pj=pj: e.matmul(pj[:], lhsT=src[:, kc, t * 128:(t + 1) * 128], rhs=ws[:, kc, :],
                                                       start=(kc == 0), stop=(kc == 7)),
                  reads=[bws, bsrc(kc)], writes=[bpj])
        P.add(evac_eng, lambda e, pj=pj: evac(e, pj), reads=[bpj] + ereads, writes=ewrites)

    def gates(ntile, d, need_o):
        n = ntile * 128
        ep = elpar[0]
        proj_fm(WAF, bf("WAF"), 0, n, "dve", lambda e, pj: e.tensor_copy(out=AFAB[0:32, 0:n], in_=pj[0:32, 0:n]),
                [], [bf("AFAB")], M=32, lhs_cols=(0, 32))
        gps = []
        for t in range(ntile):
            pj, bpj = next_pj()
            P.add("pe", lambda e, t=t, pj=pj: e.matmul(pj[:], lhsT=AFAB[0:33, t * 128:(t + 1) * 128], rhs=WAL[d][0:33, :],
                                                     start=True, stop=True), reads=[bf("AFAB"), bf("WAL")], writes=[bpj])
            P.add("act", lambda e, t=t, pj=pj: e.activation(out=G_[:, t, :], in_=pj[:], func=AF.Exp, scale=-1.0),
                  reads=[bpj], writes=[bf("G%d" % t)])
            P.add("act", lambda e, t=t: e.activation(out=G_[:, t, :], in_=G_[:, t, :], func=AF.Ln, bias=1.0),
                  reads=[bf("G%d" % t)], writes=[bf("G%d" % t)])
            gp = t
            gps.append(gp)
            bgh = bf("GHL%d" % gp)
            P.add("dve", lambda e, t=t, gp=gp: e.tensor_copy(out=GHL[:, gp, 0, :], in_=G_[:, t, :]), reads=[bf("G%d" % t)], writes=[bgh])
            P.add("dve", lambda e, t=t, gp=gp: e.tensor_tensor(out=GHL[:, gp, 1, :], in0=G_[:, t, :], in1=GHL[:, gp, 0, :], op=ALU.subtract),
                  reads=[bf("G%d" % t), bgh], writes=[bgh])
        for t in range(ntile):
            gp = gps[t]
            bgh = bf("GHL%d" % gp)
            pj, bpj = next_pj()
            for h in range(4):
                for hl in range(2):
                    P.add("pe", lambda e, h=h, hl=hl, gp=gp, pj=pj: e.matmul(pj[:, h * 128:(h + 1) * 128], lhsT=GHL[:, gp, hl, h * 128:(h + 1) * 128],
                                                                          rhs=TRIb[d][:], start=(hl == 0), stop=(hl == 1)),
                          reads=[bgh, cst], writes=[bpj])
            pj3 = pj[:].rearrange("p (h c) -> p h c", c=128)
            if need_o:
                P.add("act", lambda e, t=t, pj3=pj3: e.activation(out=E_[:, :, t * 128:(t + 1) * 128], in_=pj3, func=AF.Exp),
                      reads=[bpj], writes=[bf("E%d" % t)])
                P.add("act", lambda e, t=t, pj3=pj3: e.activation(out=EI_[:, :, t * 128:(t + 1) * 128], in_=pj3, func=AF.Exp, scale=-1.0),
                      reads=[bpj], writes=[bf("EI%d" % t)])
            li = 127 if d == "f" else 0
            P.add("act", lambda e, t=t, pj3=pj3, li=li, ep=ep: e.activation(out=ELAST2[:, ep, t, :], in_=pj3[:, :, li], func=AF.Exp),
                  reads=[bpj], writes=[bf("EL%d_%d" % (ep, t))])
            pj, bpj = next_pj()
            for hl in range(2):
                P.add("pe", lambda e, hl=hl, gp=gp, pj=pj: e.matmul(pj[:], lhsT=TRISb[d][:], rhs=GHL[:, gp, hl, :], start=(hl == 0), stop=(hl == 1)),
                      reads=[bgh, cst], writes=[bpj])
            P.add("act", lambda e, t=t, pj=pj: e.activation(out=ER_[:, t, :], in_=pj[:], func=AF.Exp), reads=[bpj], writes=[bf("ER%d" % t)])

    def gla_mults(ntile):
        n = ntile * 128
        for h in range(4):
            P.add("dve", lambda e, h=h: e.tensor_tensor(out=QD[:, h, 0:n], in0=QD[:, h, 0:n], in1=E_[:, h, 0:n], op=ALU.mult),
                  reads=[bf("QD%d" % h)] + bl("E", ntile), writes=[bf("QD%d" % h)])
        for h in range(4):
            P.add("dve", lambda e, h=h: e.tensor_tensor(out=KI[:, h, 0:n], in0=KI[:, h, 0:n], in1=EI_[:, h, 0:n], op=ALU.mult),
                  reads=[bf("KI%d" % h)] + bl("EI", ntile), writes=[bf("KI%d" % h)])
        for t in range(ntile):
            P.add("dve", lambda e, t=t: e.tensor_tensor(out=KST[:, t, :], in0=KST[:, t, :], in1=ER_[:, t, :], op=ALU.mult),
                  reads=[bf("KST%d" % t), bf("ER%d" % t)], writes=[bf("KST%d" % t)])

    PRJ_PARTS = [(0, 2048, lambda: bl("QD", 4)), (2048, 4096, lambda: bl("KI", 4)), (4096, 6144, lambda: bl("KST", 4)),
                 (6144, 10240, lambda: [bf("VV%d_%d" % (t, hf)) for t in range(4) for hf in range(2)])]

    def gla_inputs(ntile, need_o, mode=None, g=None):
        n = ntile * 128
        if mode == "B":
            for pi, (c0, c1, bufs) in enumerate(PRJ_PARTS):
                P.add("sp", lambda e, c0=c0, c1=c1: e.dma_start(out=GLA[:, c0:c1], in_=prj[g][:, c0:c1]),
                      reads=[bPRJ[g][pi]], writes=bufs(), dma=True)
            gla_mults(ntile)
            return
        raw = mode == "A"
        if need_o:
            ws, bws = load_slot(SL["q"][0])
            for h in range(4):
                if raw:
                    ev = lambda e, pj, h=h: e.tensor_scalar(out=QD[:, h, 0:n], in0=pj[:, 0:n], scalar1=128.0 ** -0.5, scalar2=None, op0=ALU.mult)
                    rd = []
                else:
                    ev = lambda e, pj, h=h: e.scalar_tensor_tensor(out=QD[:, h, 0:n], in0=pj[:, 0:n], scalar=128.0 ** -0.5,
                                                                   in1=E_[:, h, 0:n], op0=ALU.mult, op1=ALU.mult)
                    rd = bl("E", ntile)
                proj_fm(ws, bws, h, n, "dve", ev, rd, [bf("QD%d" % h)])
        ws, bws = load_slot(SL["k"][0])
        if need_o:
            for h in range(4):
                if raw:
                    proj_fm(ws, bws, h, n, "act", lambda e, pj, h=h: e.activation(out=KI[:, h, 0:n], in_=pj[:, 0:n], func=AF.Copy),
                            [], [bf("KI%d" % h)])
                else:
                    proj_fm(ws, bws, h, n, "dve",
                            lambda e, pj, h=h: e.tensor_tensor(out=KI[:, h, 0:n], in0=pj[:, 0:n], in1=EI_[:, h, 0:n], op=ALU.mult),
                            bl("EI", ntile), [bf("KI%d" % h)])
        for t in range(ntile):
            if raw:
                proj_tm(ws, bws, t, "dve", lambda e, pj, t=t: e.tensor_copy(out=KST[:, t, :], in_=pj[:]), [], [bf("KST%d" % t)])
            else:
                proj_tm(ws, bws, t, "dve", lambda e, pj, t=t: e.tensor_tensor(out=KST[:, t, :], in0=pj[:], in1=ER_[:, t, :], op=ALU.mult),
                        [bf("ER%d" % t)], [bf("KST%d" % t)])
        for hf in range(2):
            ws, bws = load_slot(SL["vv"][hf])
            for t in range(ntile):
                proj_tm(ws, bws, t, "act", lambda e, pj, t=t, hf=hf: e.activation(out=VV[:, t, hf * 512:(hf + 1) * 512], in_=pj[:], func=AF.Copy),
                        [], [bf("VV%d_%d" % (t, hf))])
        if raw:
            for pi, (c0, c1, bufs) in enumerate(PRJ_PARTS):
                P.add("sp", lambda e, c0=c0, c1=c1: e.dma_start(out=prj[g][:, c0:c1], in_=GLA[:, c0:c1]),
                      reads=bufs(), writes=[bPRJ[g][pi]], dma=True)
            gla_mults(ntile)

    def gla_tile(t, d, need_o, o_done, ep):
        vvb = [bf("VV%d_0" % t), bf("VV%d_1" % t)]
        cur = spar[0]
        nxt = 1 - cur
        spar[0] = nxt
        if need_o:
            pj, bpj = next_pj()
            for h in range(4):
                P.add("pe", lambda e, h=h, pj=pj: e.matmul(pj[:, h * 128:(h + 1) * 128], lhsT=KI[:, h, t * 128:(t + 1) * 128],
                                                         rhs=QD[:, h, t * 128:(t + 1) * 128], start=True, stop=True),
                      reads=[bf("KI%d" % h), bf("QD%d" % h)], writes=[bpj])
            P.add("dve", lambda e, pj=pj: e.tensor_tensor(out=ATT[:], in0=pj[:].rearrange("p (h c) -> p h c", c=128),
                                                        in1=MASK[d][:].unsqueeze(1).broadcast_to([128, 4, 128]), op=ALU.mult),
                  reads=[bpj, cst], writes=[bf("ATT")])
        for h in range(4):
            pd = PO[2 + h // 2][:, (h % 2) * 256:(h % 2) * 256 + 256]
            P.add("pe", lambda e, h=h, pd=pd: e.matmul(pd, lhsT=KST[:, t, h * 128:(h + 1) * 128], rhs=VV[:, t, h * 256:(h + 1) * 256],
                                                     start=True, stop=True), reads=[bf("KST%d" % t), vvb[h // 2]], writes=[bPO[2 + h // 2]])
        for h in range(4):
            pd = PO[2 + h // 2][:, (h % 2) * 256:(h % 2) * 256 + 256]
            P.add("dve", lambda e, h=h, pd=pd: e.scalar_tensor_tensor(out=S32[:, h, :], in0=S32[:, h, :], scalar=ELAST2[:, ep, t, h:h + 1],
                                                                    in1=pd, op0=ALU.mult, op1=ALU.add),
                  reads=[bPO[2 + h // 2], bf("EL%d_%d" % (ep, t)), bf("S32")], writes=[bf("S32")])
        P.add("pool", lambda e: e.tensor_copy(out=SBF2[:, nxt, :, :], in_=S32[:]), reads=[bf("S32")], writes=[bf("SBFp%d" % nxt)])
        if need_o:
            for h in range(4):
                po = PO[h // 2][:, (h % 2) * 256:(h % 2) * 256 + 256]
                P.add("pe", lambda e, h=h, po=po: e.matmul(po, lhsT=ATT[:, h, :], rhs=VV[:, t, h * 256:(h + 1) * 256],
                                                         start=True, stop=False), reads=[bf("ATT"), vvb[h // 2]], writes=[bPO[h // 2]])
                P.add("pe", lambda e, h=h, po=po: e.matmul(po, lhsT=QD[:, h, t * 128:(t + 1) * 128], rhs=SBF2[:, cur, h, :],
                                                         start=False, stop=True), reads=[bf("QD%d" % h), bf("SBFp%d" % cur)], writes=[bPO[h // 2]])
            o_done(t)

    NG = 8
    ngA = 1 if stop_after == "A1" else NG
    seqA = [(ctx, 2, S1C, SH1C, False, 0)] + [(x[g * 512:(g + 1) * 512, :], 4, S1, SH1, True, g * 512) for g in reversed(range(NG - ngA, NG))]

    def inputsA(item):
        if item[4]:
            gla_inputs(item[1], True, mode="A", g=item[5] // 512)
        else:
            gla_inputs(item[1], False)


    def xloadA(item):
        src_ap, ntile = item[0], item[1]
        for t in range(ntile):
            P.add("sp", lambda e, t=t: e.dma_start(out=X[:, t, :], in_=src_ap[t * 128:(t + 1) * 128, :]),
                  writes=[bf("X%d" % t)], dma=True)

    def preA(item):
        src_ap, ntile, Sv, SHv, need_o, row0 = item
        norm_to_HT(ntile, Sv, SHv)
        gates(ntile, "b", need_o)

    def tilesA(item, ep):
        src_ap, ntile, Sv, SHv, need_o, row0 = item

        def o_done(t):
            for i in range(2):
                P.add("act", lambda e, i=i: e.activation(out=O[:, i * 512:(i + 1) * 512], in_=PO[i][:], func=AF.Copy),
                      reads=[bPO[i]], writes=[bf("O")])
            r0 = row0 + t * 128
            P.add("pool", lambda e: e.dma_start(out=obs[r0:r0 + 128, :], in_=O[:]), reads=[bf("O")], writes=[bOBS[r0 // 128]], dma=True)

        for t in reversed(range(ntile)):
            gla_tile(t, "b", need_o, o_done, ep)

    P.add("pool", lambda e: e.memset(S32[:], 0.0), writes=[bf("S32")])
    P.add("pool", lambda e: e.memset(SBF2[:], 0.0), writes=[bf("SBFp0"), bf("SBFp1")])
    elpar[0] = 0
    xloadA(seqA[0])
    preA(seqA[0])
    if len(seqA) > 1:
        xloadA(seqA[1])
    inputsA(seqA[0])
    issue_casts(4)
    for i in range(1, len(seqA)):
        pj_pool[0] = [3]
        P.capture(); tilesA(seqA[i - 1], (i - 1) % 2); sT = P.end_capture()
        pj_pool[0] = [0, 1, 2]
        elpar[0] = i % 2
        P.capture(); preA(seqA[i]); sF = P.end_capture()
        P.commit(sT, sF)
        pj_pool[0] = [0, 1, 2, 3]
        if i + 1 < len(seqA):
            xloadA(seqA[i + 1])
        inputsA(seqA[i])
        issue_casts(4)
    tilesA(seqA[-1], (len(seqA) - 1) % 2)
    issue_casts(100)
    if stop_after == "A1":
        tap("ob", O[:], [bf("O")])
        P.emit()
        return nc, P

    def sweepB_ctx():
        P.add("pool", lambda e: e.memset(S32[:], 0.0), writes=[bf("S32")])
        P.add("pool", lambda e: e.memset(SBF2[:], 0.0), writes=[bf("SBFp0"), bf("SBFp1")])
        elpar[0] = 0
        front(ctx, 2, S1C, SH1C, None)
        gates(2, "f", False)
        gla_inputs(2, False)
        for t in range(2):
            gla_tile(t, "f", False, None, 0)

    def load_X(g):
        for t in range(4):
            P.add("sp", lambda e, t=t: e.dma_start(out=X[:, t, :], in_=x[g * 512 + t * 128:g * 512 + (t + 1) * 128, :]),
                  writes=[bf("X%d" % t)], dma=True)

    def front_early(g, part):
        if part == 2:
            xn_to_fm(4, lambda e, kc, pt, n: e.tensor_scalar(out=HT[:, kc, 0:n], in0=pt[:, 0:n], scalar1=S1[:, kc:kc + 1],
                                                             scalar2=SH1[:, kc:kc + 1], op0=ALU.mult, op1=ALU.add),
                     lambda kc: [bf("S"), bf("MODF")], lambda kc: [bf("HT%d" % kc)],
                     act_evac=lambda e, kc, pt, n: e.activation(out=HT[:, kc, 0:n], in_=pt[:, 0:n], func=AF.Identity,
                                                                scale=S1[:, kc:kc + 1], bias=SH1[:, kc:kc + 1]))
            return
        stg = [(O, bf("O")), (OB, bf("OB"))]
        for t in range(4):
            S_, bS = stg[t % 2]
            P.add("sp", lambda e, t=t, S_=S_: e.dma_start(out=S_[:], in_=x[g * 512 + t * 128:g * 512 + (t + 1) * 128, :]),
                  writes=[bS], dma=True)
            P.add("act", lambda e, t=t, S_=S_: e.activation(out=XN[:, t, :], in_=S_[:], func=AF.Square, accum_out=STAT[:, t:t + 1]),
                  reads=[bS], writes=[bf("STa%d" % t), bf("XN%d" % t)])
            P.add("dve", lambda e, t=t: e.tensor_scalar(out=STAT[:, 4 + t:5 + t], in0=STAT[:, t:t + 1], scalar1=1.0 / D, scalar2=1e-6,
                                                        op0=ALU.mult, op1=ALU.add), reads=[bf("STa%d" % t)], writes=[bf("STb%d" % t)])
            P.add("pool", lambda e, t=t: e.tensor_tensor(out=STAT[:, 8 + t:9 + t], in0=STAT[:, 4 + t:5 + t], in1=NHALF[:, 0:1], op=ALU.pow),
                  reads=[bf("STb%d" % t), cst], writes=[bf("STc%d" % t)])
            P.add("dve", lambda e, t=t, S_=S_: e.tensor_scalar(out=XN[:, t, :], in0=S_[:], scalar1=STAT[:, 8 + t:9 + t], scalar2=None,
                                                             op0=ALU.mult), reads=[bS, bf("STc%d" % t)], writes=[bf("XN%d" % t)])

    def sweepB_group(g, ng):
        row0 = g * 512
        if g == 0:
            front_early(0, 1)
            front_early(0, 2)
            load_X(0)
        elpar[0] = 0
        gates(4, "f", True)
        pj_pool[0] = [0, 1]
        ws_pool[0] = [0, 1]
        P.capture()
        gla_inputs(4, True, mode="B", g=g)

        def o_done(t):
            r0 = row0 + t * 128
            P.add("pool", lambda e: e.dma_start(out=OB[:], in_=obs[r0:r0 + 128, :]), reads=[bOBS[r0 // 128]], writes=[bf("OB")], dma=True)
            for i in range(2):
                P.add("dve", lambda e, i=i: e.tensor_tensor(out=O[:, i * 512:(i + 1) * 512], in0=PO[i][:], in1=OB[:, i * 512:(i + 1) * 512],
                                                          op=ALU.add), reads=[bPO[i], bf("OB")], writes=[bf("O")])
            for h in range(4):
                P.add("act", lambda e, h=h: e.activation(out=XN[:, t, h * 256:(h + 1) * 256], in_=O[:, h * 256:(h + 1) * 256], func=AF.Square,
                                                         accum_out=STAT[:, 12 + h:13 + h]), reads=[bf("O")], writes=[bf("STATO"), bf("XN%d" % t)])
            P.add("dve", lambda e: e.tensor_scalar(out=STAT[:, 12:16], in0=STAT[:, 12:16], scalar1=1.0 / 256, scalar2=1e-6,
                                                   op0=ALU.mult, op1=ALU.add), reads=[bf("STATO")], writes=[bf("STATO")])
            P.add("pool", lambda e: e.tensor_tensor(out=STAT[:, 12:16], in0=STAT[:, 12:16], in1=NHALF[:, 0:4], op=ALU.pow),
                  reads=[bf("STATO"), cst], writes=[bf("STATO")])
            for h in range(4):
                P.add("dve", lambda e, h=h: e.tensor_scalar(out=XN[:, t, h * 256:(h + 1) * 256], in0=O[:, h * 256:(h + 1) * 256],
                                                            scalar1=STAT[:, 12 + h:13 + h], scalar2=None, op0=ALU.mult),
                      reads=[bf("O"), bf("STATO")], writes=[bf("XN%d" % t)])

        for t in range(4):
            gla_tile(t, "f", True, o_done, 0)
        s1 = P.end_capture()
        pj_pool[0] = [2, 3]
        ws_pool[0] = [0, 1, 2, 3]
        P.capture()
        for hf in range(2):
            ws, bws = load_slot(SL["r"][hf])
            for c in range(4):
                proj_fm(ws, bws, c, 512, "act", lambda e, pj, c=c, hf=hf: e.activation(out=RT[:, hf * 4 + c, :], in_=pj[:], func=AF.Silu),
                        [], [bf("RT%d" % (hf * 4 + c))])
        for hf in range(2):
            ws, bws = load_slot(SL["u"][hf])
            for c in range(4):
                proj_fm(ws, bws, c, 512, "act", lambda e, pj, c=c, hf=hf: e.activation(out=UT[:, hf * 4 + c, :], in_=pj[:], func=AF.Gelu),
                        [], [bf("UT%d" % (hf * 4 + c))])
        wsv = [load_slot(SL["v"][hf]) for hf in range(2)]
        def vproj(t):
            gq = t % 2
            GVq = GV2[:, gq, :]
            gvb = [bf("GV%d_%d" % (gq, 0)), bf("GV%d_%d" % (gq, 1))]
            bns, bna = bf("BNS%d" % gq), bf("BNA%d" % gq)
            for hf in range(2):
                proj_tm(wsv[hf][0], wsv[hf][1], t, "act",
                        lambda e, pj, hf=hf, GVq=GVq: e.activation(out=GVq[:, hf * 512:(hf + 1) * 512], in_=pj[:], func=AF.Gelu), [], [gvb[hf]])

        def ln_spatial(t):
            gq = t % 2
            GVq = GV2[:, gq, :]
            gvb = [bf("GV%d_%d" % (gq, 0)), bf("GV%d_%d" % (gq, 1))]
            bns, bna = bf("BNS%d" % gq), bf("BNA%d" % gq)
            for hf in range(2):
                P.add("dve", lambda e, hf=hf, GVq=GVq, gq=gq: e.bn_stats(out=BNS[:, gq, hf, :], in_=GVq[:, hf * 512:(hf + 1) * 512]), reads=[gvb[hf]], writes=[bns])
            bmu = bf("BMU%d" % gq)
            P.add("dve", lambda e, gq=gq: e.bn_aggr(out=BNA[:, gq, :], in_=BNS[:, gq, :, :].rearrange("p a b -> p (a b)")), reads=[bns], writes=[bna, bmu])
            P.add("dve", lambda e, gq=gq: e.tensor_scalar(out=BNA[:, gq, 1:2], in0=BNA[:, gq, 1:2], scalar1=1e-6, scalar2=None, op0=ALU.add),
                  reads=[bna], writes=[bna])
            P.add("pool", lambda e, gq=gq: e.tensor_tensor(out=BNA[:, gq, 1:2], in0=BNA[:, gq, 1:2], in1=NHALF[:, 0:1], op=ALU.pow),
                  reads=[bna, cst], writes=[bna])
            P.add("dve", lambda e, GVq=GVq, gq=gq: e.scalar_tensor_tensor(out=GVq, in0=GVq, scalar=BNA[:, gq, 0:1], in1=LNG[:], op0=ALU.subtract, op1=ALU.mult),
                  reads=gvb + [bmu, cst], writes=gvb)
            P.add("dve", lambda e, t=t, GVq=GVq, gq=gq: e.scalar_tensor_tensor(out=VN[:, t, :], in0=GVq, scalar=BNA[:, gq, 1:2], in1=LNB[:], op0=ALU.mult, op1=ALU.add),
                  reads=gvb + [bna, cst], writes=[bf("VN%d" % t)])
            for half in range(2):
                pj, bpj = next_pj()
                P.add("pe", lambda e, half=half, pj=pj: e.matmul(pj[:], lhsT=ONESR[0:1, :], rhs=BSROW[0:1, half * 512:(half + 1) * 512],
                                                               start=True, stop=False, skip_group_check=True),
                      reads=[cst], writes=[bpj])
                for gg in range(4):
                    g8 = half * 4 + gg
                    P.add("pe", lambda e, g8=g8, gg=gg, pj=pj, t=t: e.matmul(pj[:, gg * 128:(gg + 1) * 128], lhsT=VN[:, t, g8 * 128:(g8 + 1) * 128],
                                                                          rhs=WSPT[:, g8, :], start=False, stop=(gg == 3), skip_group_check=True),
                          reads=[bf("VN%d" % t), cst], writes=[bpj])
                utb = [bf("UT%d" % (half * 4 + gg)) for gg in range(4)]
                P.add("dve", lambda e, half=half, pj=pj, t=t: e.tensor_tensor(
                    out=UT[:, half * 4:half * 4 + 4, t * 128:(t + 1) * 128], in0=pj[:].rearrange("p (g c) -> p g c", c=128),
                    in1=UT[:, half * 4:half * 4 + 4, t * 128:(t + 1) * 128], op=ALU.mult), reads=[bpj] + utb, writes=utb)

        vproj(0)
        for t in range(4):
            if t + 1 < 4:
                vproj(t + 1)
            ln_spatial(t)
        s2 = P.end_capture()
        kk = min(len(s2), 2 * len(s1))
        P.commit(s1, s2[:kk])
        P.commit(s2[kk:])
        pj_pool[0] = [0, 1, 2, 3]
        ws_pool[0] = [0, 1, 2, 3]
        def tab(nm, c8):
            if nm == "ga":
                return [bf("QD%d" % c8)] if c8 < 4 else [bf("KI%d" % (c8 - 4))]
            return [bf("KST%d" % c8)] if c8 < 4 else [bf("VV%d_0" % ((c8 - 4) // 2)), bf("VV%d_1" % ((c8 - 4) // 2))]
        for nm, dst in (("ga", TA), ("gb", TB)):
            for hf in range(2):
                ws, bws = load_slot(SL[nm][hf])
                for c in range(4):
                    proj_fm(ws, bws, c, 512, "act",
                            lambda e, pj, c=c, hf=hf, dst=dst: e.activation(out=dst[:, hf * 4 + c, :], in_=pj[:], func=AF.Tanh, scale=0.5),
                            [], tab(nm, hf * 4 + c))
        xn_to_fm(4, lambda e, kc, pt, n: e.scalar_tensor_tensor(out=RT[:, kc, :], in0=pt[:, 0:n], scalar=VFM[:, 2, kc:kc + 1],
                                                                in1=RT[:, kc, :], op0=ALU.mult, op1=ALU.mult),
                 lambda kc: [bf("VFM"), bf("RT%d" % kc)], lambda kc: [bf("RT%d" % kc)])
        wsa = [load_slot(SL["ba"][hf]) for hf in range(2)]
        wsb = [load_slot(SL["bb"][hf]) for hf in range(2)]
        for c8 in range(8):
            hf, c = divmod(c8, 4)
            pja, bpja = next_pj()
            for kc in range(8):
                P.add("pe", lambda e, kc=kc, pja=pja, hf=hf, c=c: e.matmul(pja[:], lhsT=wsa[hf][0][:, kc, c * 128:(c + 1) * 128], rhs=UT[:, kc, :],
                                                                         start=(kc == 0), stop=(kc == 7)), reads=[wsa[hf][1], bf("UT%d" % kc)], writes=[bpja])
            P.add("dve", lambda e, pja=pja, c8=c8: e.scalar_tensor_tensor(out=TMP[:], in0=TA[:, c8, :], scalar=1.0, in1=pja[:], op0=ALU.add, op1=ALU.mult),
                  reads=[bpja] + tab("ga", c8), writes=[bf("TMP")])
            pjb, bpjb = next_pj()
            for kc in range(8):
                P.add("pe", lambda e, kc=kc, pjb=pjb, hf=hf, c=c: e.matmul(pjb[:], lhsT=wsb[hf][0][:, kc, c * 128:(c + 1) * 128], rhs=RT[:, kc, :],
                                                                         start=(kc == 0), stop=(kc == 7)), reads=[wsb[hf][1], bf("RT%d" % kc)], writes=[bpjb])
            P.add("dve", lambda e, pjb=pjb, c8=c8: e.scalar_tensor_tensor(out=GV[:, 0:512], in0=TB[:, c8, :], scalar=1.0, in1=pjb[:], op0=ALU.add, op1=ALU.mult),
                  reads=[bpjb] + tab("gb", c8), writes=[bf("GV0_0")])
            P.add("dve", lambda e, c8=c8: e.tensor_tensor(out=YT[:, c8, :], in0=TMP[:], in1=GV[:, 0:512], op=ALU.add),
                  reads=[bf("TMP"), bf("GV0_0")], writes=[bf("VN%d" % (c8 // 2))])
        wso = [load_slot(SL["wo"][hf]) for hf in range(2)]
        for t in range(4):
            for hf in range(2):
                def ev(e, pj, t=t, hf=hf):
                    return e.tensor_tensor(out=TMP[:], in0=pj[:], in1=G1H[:, hf * 512:(hf + 1) * 512], op=ALU.mult)
                proj_tm(wso[hf][0], wso[hf][1], t, "dve", ev, [cst], [bf("TMP")], src=YT, bsrc=lambda kc: bf("VN%d" % (kc // 2)))
                P.add("dve", lambda e, t=t, hf=hf: e.tensor_tensor(out=X[:, t, hf * 512:(hf + 1) * 512], in0=X[:, t, hf * 512:(hf + 1) * 512],
                                                                 in1=TMP[:], op=ALU.add), reads=[bf("TMP"), bf("X%d" % t)], writes=[bf("X%d" % t)])
        norm_to_HT(4, S2, SH2)
        for j in range(11):
            ws, bws = load_slot(20 + j)
            for sub in range(2):
                pa, bpa = next_pj()
                for kc in range(8):
                    P.add("pe", lambda e, kc=kc, pa=pa, sub=sub, ws=ws: e.matmul(pa[:], lhsT=ws[:, kc, sub * 128:(sub + 1) * 128], rhs=HT[:, kc, :],
                                                                               start=(kc == 0), stop=(kc == 7)), reads=[bws, bf("HT%d" % kc)], writes=[bpa])
                pg, bpg = next_pj()
                for kc in range(8):
                    P.add("pe", lambda e, kc=kc, pg=pg, sub=sub, ws=ws: e.matmul(pg[:], lhsT=ws[:, kc, 256 + sub * 128:256 + (sub + 1) * 128], rhs=HT[:, kc, :],
                                                                               start=(kc == 0), stop=(kc == 7)), reads=[bws, bf("HT%d" % kc)], writes=[bpg])
                P.add("act", lambda e, pg=pg: e.activation(out=TMP[:], in_=pg[:], func=AF.Silu), reads=[bpg], writes=[bf("TMP")])
                P.add("dve", lambda e, pa=pa, j=j, sub=sub: e.tensor_tensor(out=ACTT[:, 2 * j + sub, :], in0=pa[:], in1=TMP[:], op=ALU.mult),
                      reads=[bpa, bf("TMP")], writes=[bf("ACTT%d" % (2 * j + sub))] + GALIAS)
        P.capture()
        for hf in range(2):
            for pc in range(3):
                nk = 8 if pc < 2 else 6
                ws, bws = load_slot(31 + hf * 3 + pc, nk * 512)
                for t in range(4):
                    for k in range(nk):
                        kc = pc * 8 + k
                        P.add("pe", lambda e, t=t, k=k, kc=kc, ws=ws: e.matmul(PO[t][:], lhsT=ACTT[:, kc, t * 128:(t + 1) * 128], rhs=ws[:, k, :],
                                                                             start=(kc == 0), stop=(kc == 21)),
                              reads=[bws, bf("ACTT%d" % kc)] + (GALIAS if (hf == 1 and t == 3) else []), writes=[bPO[t]])
            for t in range(4):
                P.add("dve", lambda e, t=t, hf=hf: e.tensor_tensor(out=TMP[:], in0=PO[t][:], in1=G2B[:, hf * 512:(hf + 1) * 512], op=ALU.mult),
                      reads=[bPO[t], cst], writes=[bf("TMP")])
                P.add("dve", lambda e, t=t, hf=hf: e.tensor_tensor(out=X[:, t, hf * 512:(hf + 1) * 512], in0=X[:, t, hf * 512:(hf + 1) * 512],
                                                                 in1=TMP[:], op=ALU.add), reads=[bf("TMP"), bf("X%d" % t)], writes=[bf("X%d" % t)])
        sFO = P.end_capture()
        if g + 1 < ng:
            pj_pool[0] = [0, 1, 2, 3]
            P.capture()
            front_early(g + 1, 1)
            sFE = P.end_capture()
            P.commit(sFO, sFE)
            front_early(g + 1, 2)
        else:
            P.commit(sFO)
        UTJ = UT[:, 0:2, :].rearrange("p a b -> p (a b)")
        for t in range(4):
            P.add("act", lambda e, t=t: e.activation(out=UTJ, in_=X[:, t, :], func=AF.Square, accum_out=STAT[:, t:t + 1]),
                  reads=[bf("X%d" % t)], writes=[bf("STa%d" % t), bf("UT0"), bf("UT1")])
        P.add("dve", lambda e: e.tensor_scalar(out=STAT[:, 4:8], in0=STAT[:, 0:4], scalar1=1.0 / D, scalar2=1e-6, op0=ALU.mult, op1=ALU.add),
              reads=bl("STa", 4), writes=[bf("STb")] + bl("STb", 4))
        P.add("pool", lambda e: e.tensor_tensor(out=STAT[:, 8:12], in0=STAT[:, 4:8], in1=NHALF[:, 0:4], op=ALU.pow),
              reads=[bf("STb"), cst], writes=[bf("STc")] + bl("STc", 4))
        for t in range(4):
            P.add("dve", lambda e, t=t: e.scalar_tensor_tensor(out=X[:, t, :], in0=X[:, t, :], scalar=STAT[:, 8 + t:9 + t], in1=FNG[:],
                                                               op0=ALU.mult, op1=ALU.mult), reads=[bf("X%d" % t), bf("STc"), cst], writes=[bf("X%d" % t)])
            P.add("sp", lambda e, t=t: e.dma_start(out=out[row0 + t * 128:row0 + (t + 1) * 128, :], in_=X[:, t, :]),
                  reads=[bf("X%d" % t)], writes=[bf("OUT%d" % t)], dma=True)
        if g + 1 < ng:
            load_X(g + 1)

    sweepB_ctx()
    ng = {None: NG, "B_2": 2}.get(stop_after, 1)
    for g in range(ng):
        sweepB_group(g, ng)
    P.emit()
    return nc, P


def make_in_maps(inp, cores=range(8)):
    f = lambda a: np.ascontiguousarray(a, dtype=np.float32)
    vecs = np.stack([inp["norm1_g"][0], inp["norm2_g"][0], inp["gla_norm_g"][0]], 0)
    vecs_fm = f(vecs.reshape(3, 8, 128).transpose(2, 0, 1))
    rows = f(np.stack([inp["final_norm_g"], inp["ln_v_g"][0], inp["ln_v_b"][0], inp["b_spatial"][0].reshape(-1)], 0))
    walpha = f(np.stack([np.concatenate([inp["w_alpha_f"][0], inp["b_alpha_f"]], 0),
                         np.concatenate([inp["w_alpha_b"][0], inp["b_alpha_b"]], 0)], 0))
    shared = dict(w_ada=f(inp["w_ada"][0]), b_ada=f(inp["b_ada"]), vecs_fm=vecs_fm, rows=rows, w_in=f(inp["w_in"][0]),
                  w_spatial=f(inp["w_spatial"][0]), walpha=walpha, w_branch_a=f(inp["w_branch_a"][0]),
                  w_branch_b=f(inp["w_branch_b"][0]), w_out=f(inp["w_out"][0]), w_ffn_in=f(inp["w_ffn_in"][0]),
                  w_ffn_out=f(inp["w_ffn_out"][0]))
    maps = []
    for b in cores:
        m = dict(shared)
        m["x"] = f(inp["x"][b])
        m["ctx"] = f(inp["ctx"][b])
        m["c2"] = f(np.stack([inp["c"][b], inp["c_ctx"]], 1))
        maps.append(m)
    return maps


_NC_CACHE = {}


def kernel(**inputs):
    inp = {k: np.asarray(v) for k, v in inputs.items()}
    if "nc" not in _NC_CACHE:
        _NC_CACHE["nc"] = build_nc()[0]
    nc = _NC_CACHE["nc"]
    maps = make_in_maps(inp)
    res = run_bass_kernel_spmd(nc, maps, core_ids=list(range(8)))
    return np.stack([np.asarray(r["out"], dtype=np.float32) for r in res.results], 0)
```

```python
import contextlib
import numpy as np
import concourse.bass as bass
import concourse.mybir as mybir
from concourse.bass_utils import run_bass_kernel_spmd

F32 = mybir.dt.float32
BF16 = mybir.dt.bfloat16
AF = mybir.ActivationFunctionType
ALU = mybir.AluOpType

N_DMA_SEMS = 32
T_LAT = 4096
T_CTX = 256
D = 1024
DFF = 2816
NSLOT = 37
RING = 4


class Buf:
    __slots__ = ("name", "last_w", "readers")

    def __init__(self, name=""):
        self.name = name
        self.last_w = None
        self.readers = []


class Op:
    __slots__ = ("eng", "fn", "deps", "pos", "dma", "dsem", "dval", "signal", "vc", "val", "gid")

    def __init__(self, eng, fn, dma):
        self.eng = eng
        self.fn = fn
        self.deps = {}
        self.dma = dma
        self.signal = False
        self.vc = None


class Prog:
    ENGS = ("pe", "act", "dve", "pool", "sp")

    def __init__(self, nc):
        self.nc = nc
        self.ops = []
        self.q = {e: [] for e in self.ENGS}
        self.dma_by_q = {}
        self.dma_ops = []
        self._cap = None

    def capture(self):
        self._cap = []

    def end_capture(self):
        c = self._cap
        self._cap = None
        return c

    def commit(self, *streams):
        items = []
        for si, st in enumerate(streams):
            n = len(st)
            units = []
            for it in st:
                if units and it[0] == "pe" and units[-1][-1][0] == "pe" and it[3] == units[-1][-1][3]:
                    units[-1].append(it)
                else:
                    units.append([it])
            pos = 0
            for ui, u in enumerate(units):
                items.append(((pos + 0.5 * len(u)) / n, si, ui, u))
                pos += len(u)
        items.sort(key=lambda t: (t[0], t[1], t[2]))
        for _, _, _, u in items:
            for it in u:
                self.add(*it)

    def add(self, eng, fn, reads=(), writes=(), dma=False):
        if self._cap is not None:
            self._cap.append((eng, fn, tuple(reads), tuple(writes), dma))
            return None
        op = Op(eng, fn, dma)
        op.gid = len(self.ops)
        for b in reads:
            if b.last_w is not None:
                op.deps[b.last_w] = True
            b.readers.append(op)
        for b in writes:
            if b.last_w is not None and b.last_w not in op.deps:
                op.deps[b.last_w] = False
            for r in b.readers:
                if r is not op and r not in op.deps:
                    op.deps[r] = False
            b.last_w = op
            b.readers = []
        if dma:
            lst = self.dma_by_q.setdefault(eng, [])
            i = len(lst)
            half = N_DMA_SEMS // 2
            base = 0 if eng == "sp" else half
            op.dsem = base + i % half
            op.dval = 16 * (i // half + 1)
            if i >= half:
                op.deps[lst[i - half]] = True
            lst.append(op)
            self.dma_ops.append(op)
        op.pos = len(self.q[eng])
        self.q[eng].append(op)
        self.ops.append(op)
        return op

    def emit(self):
        nc = self.nc
        know = {e: {} for e in self.ENGS}
        waits = {}
        for op in self.ops:
            K = know[op.eng]
            wl = []
            best = {}
            for d, raw in op.deps.items():
                if d.dma:
                    key = ("d", d.dsem)
                    need = d.dval
                else:
                    if d.eng == op.eng and not op.dma:
                        if op.eng == "pe":
                            continue
                    key = d.eng
                    need = d.pos
                if key not in best or best[key][0] < need:
                    best[key] = (need, d)
            for key, (need, d) in best.items():
                if K.get(key, -1) >= need:
                    continue
                wl.append(d)
                d.signal = True
                for k2, v2 in d.vc.items():
                    if K.get(k2, -1) < v2:
                        K[k2] = v2
            waits[op.gid] = wl
            vc = dict(K)
            if op.dma:
                vc[("d", op.dsem)] = op.dval
            else:
                vc[op.eng] = op.pos
            op.vc = vc
        for e in self.ENGS:
            c = 0
            for op in self.q[e]:
                if op.dma:
                    continue
                if op.signal:
                    c += 1
                    op.val = c
        self.n_waits = sum(len(w) for w in waits.values())
        with contextlib.ExitStack() as st:
            csem = {e: st.enter_context(nc.semaphore("cs_" + e)) for e in self.ENGS}
            dsem = [st.enter_context(nc.semaphore("ds%d" % i)) for i in range(N_DMA_SEMS)]
            block = st.enter_context(nc.Block())

            def run(ename):
                def body(eng):
                    for op in self.q[ename]:
                        for d in waits[op.gid]:
                            if d.dma:
                                eng.wait_ge(dsem[d.dsem], d.dval)
                            else:
                                eng.wait_ge(csem[d.eng], d.val)
                        ins = op.fn(eng)
                        if op.dma:
                            ins.then_inc(dsem[op.dsem], 16)
                        elif op.signal:
                            ins.then_inc(csem[op.eng], 1)
                    if ename == "sp":
                        last = {}
                        for d in self.dma_ops:
                            last[d.dsem] = max(last.get(d.dsem, 0), d.dval)
                        for s, v in last.items():
                            eng.wait_ge(dsem[s], v)

                return body

            block.tensor(run("pe"))
            block.scalar(run("act"))
            block.vector(run("dve"))
            block.gpsimd(run("pool"))
            block.sync(run("sp"))


def build_nc(dbg=None, stop_after=None):
    nc = bass.Bass("TRN2", target_bir_lowering=False)
    P = Prog(nc)

    def din(name, shape):
        return nc.dram_tensor(name, list(shape), F32, kind="ExternalInput").ap()

    x = din("x", [T_LAT, D])
    ctx = din("ctx", [T_CTX, D])
    c2 = din("c2", [D, 2])
    w_ada = din("w_ada", [D, 6 * D])
    b_ada = din("b_ada", [1, 6 * D])
    vecs_fm = din("vecs_fm", [128, 3, 8])
    rows = din("rows", [4, D])
    w_in = din("w_in", [D, 7200])
    w_spatial = din("w_spatial", [8, 128, 128])
    walpha = din("walpha", [2, 17, 512])
    w_ba = din("w_branch_a", [D, D])
    w_bb = din("w_branch_b", [D, D])
    w_out = din("w_out", [D, D])
    w_fi = din("w_ffn_in", [D, 2 * DFF])
    w_fo = din("w_ffn_out", [DFF, D])
    out = nc.dram_tensor("out", [T_LAT, D], F32, kind="ExternalOutput").ap()
    wsc = nc.dram_tensor("wsc", [NSLOT, 128, 4096], BF16).ap()
    obs = nc.dram_tensor("obs", [T_LAT, D], F32).ap()
    prj = nc.dram_tensor("prj", [8, 128, 10240], BF16).ap()
    dbg_out = {}
    if dbg:
        for k, shp in dbg.items():
            dbg_out[k] = nc.dram_tensor("dbg_" + k, list(shp), F32, kind="ExternalOutput").ap()

    def sb(name, shape, dt=F32):
        return nc.alloc_sbuf_tensor(name, list(shape), dt)

    IDF = sb("IDF", [128, 128])
    ID = sb("ID", [128, 128], BF16)
    MASK = {d: sb("MASK" + d, [128, 128], BF16) for d in "fb"}
    TRIb = {d: sb("TRIb" + d, [128, 128], BF16) for d in "fb"}
    TRISb = {d: sb("TRISb" + d, [128, 128], BF16) for d in "fb"}
    GHL = sb("GHL", [128, 4, 2, 512], BF16)
    ghl_ctr = [0]
    NHALF = sb("NHALF", [128, 8])
    ONESR = sb("ONESR", [1, 128], BF16)
    SELB = sb("SELB", [2, 128])
    SELC = sb("SELC", [2, 2])
    C2 = sb("C2", [128, 8, 2])
    SC = sb("SC", [128, 8, 2], BF16)
    VFM = sb("VFM", [128, 3, 8])
    MODF = sb("MODF", [128, 6, 8])
    S1 = sb("S1", [128, 8]); S1C = sb("S1C", [128, 8]); S2 = sb("S2", [128, 8])
    G1H = sb("G1H", [128, D]); G2B = sb("G2B", [128, D]); FNG = sb("FNG", [128, D])
    LNG = sb("LNG", [128, D]); LNB = sb("LNB", [128, D])
    BSROW = sb("BSROW", [1, D], BF16)
    WSPT = sb("WSPT", [128, 8, 128], BF16)
    WAL = {d: sb("WAL" + d, [33, 512], BF16) for d in "fb"}
    WAF = sb("WAF", [128, 8, 32], BF16)
    S32 = sb("S32", [128, 4, 256])
    SBF2 = sb("SBF2", [128, 2, 4, 256], BF16)
    spar = [0]
    WS = [sb("WS%d" % i, [128, 8, 512], BF16) for i in range(RING)]
    X = sb("X", [128, 4, D])
    XN = sb("XN", [128, 4, D], BF16)
    HT = sb("HT", [128, 8, 512], BF16)
    AFAB = sb("AFAB", [33, 512], BF16)
    STAT = sb("STAT", [128, 16])
    BNS = sb("BNS", [128, 2, 2, 6]); BNA = sb("BNA", [128, 2, 2])
    GATE = sb("GATE", [128, 4 * 4 * 512])
    MOD = GATE[0:2, 0:6 * D]
    G_ = GATE[:, 0:2048].rearrange("p (t c) -> p t c", c=512)
    E_ = GATE[:, 2048:4096].rearrange("p (h c) -> p h c", c=512)
    EI_ = GATE[:, 4096:6144].rearrange("p (h c) -> p h c", c=512)
    ER_ = GATE[:, 6144:8192].rearrange("p (t c) -> p t c", c=512)
    ACTT = GATE[:, 0:22 * 256].bitcast(BF16).rearrange("p (k c) -> p k c", c=512)
    ELAST2 = sb("ELAST", [128, 2, 4, 4])
    elpar = [0]
    GLA = sb("GLA", [128, 10240], BF16)
    QD = GLA[:, 0:2048].rearrange("p (h c) -> p h c", c=512)
    KI = GLA[:, 2048:4096].rearrange("p (h c) -> p h c", c=512)
    KST = GLA[:, 4096:6144].rearrange("p (t c) -> p t c", c=512)
    VV = GLA[:, 6144:10240].rearrange("p (t c) -> p t c", c=1024)
    TA = GLA[:, 0:4096].rearrange("p (k c) -> p k c", c=512)
    TB = GLA[:, 4096:8192].rearrange("p (k c) -> p k c", c=512)
    ATT = sb("ATT", [128, 4, 128], BF16)
    O = sb("O", [128, D]); OB = sb("OB", [128, D])
    BADA = O[0:2, :].rearrange("p (a b) -> p a b", b=512)
    TSCR = OB[:, 0:128]
    UT = sb("UT", [128, 8, 512], BF16)
    VN = sb("VN", [128, 4, D], BF16)
    YT = VN[:].rearrange("p t c -> p (t c)").rearrange("p (k c) -> p k c", c=512)
    GV2 = sb("GV2", [128, 2, D])
    GV = GV2[:, 0, :]
    WSPF = GV2[:, 0, :].rearrange("p (g q) -> p g q", q=128)
    RT = sb("RT", [128, 8, 512], BF16)
    TMP = sb("TMP", [128, 512])
    NPJ = 4
    PJ = [nc.alloc_psum_tensor("PJ%d" % i, [128, 512], F32) for i in range(NPJ)]
    PO = [nc.alloc_psum_tensor("PO%d" % i, [128, 512], F32) for i in range(4)]
    bPJ = [Buf("PJ%d" % i) for i in range(NPJ)]
    bPO = [Buf("PO%d" % i) for i in range(4)]
    pj_pool = [[0, 1, 2, 3]]
    pj_ctr = {}
    ws_pool = [[0, 1, 2, 3]]

    def next_pj():
        pool = tuple(pj_pool[0])
        c = pj_ctr.get(pool, 0)
        pj_ctr[pool] = c + 1
        i = pool[c % len(pool)]
        return PJ[i], bPJ[i]

    def next_ptr():
        pj, b = next_pj()
        return pj[:].bitcast(BF16)[:, 0:512], b

    B = {}

    def bf(name):
        if name not in B:
            B[name] = Buf(name)
        return B[name]

    bWS = [Buf("WS%d" % i) for i in range(RING)]
    bWSC = [Buf("WSC%d" % i) for i in range(NSLOT)]
    bOBS = [Buf("OBS%d" % i) for i in range(32)]
    bPRJ = [[Buf("PRJ%d_%d" % (i, j)) for j in range(4)] for i in range(8)]
    ring_ctr = {}

    def next_ring():
        pool = tuple(ws_pool[0])
        c = ring_ctr.get(pool, 0)
        ring_ctr[pool] = c + 1
        return pool[c % len(pool)]

    def tap(name, ap_sb, reads, rows_=128):
        if dbg and name in dbg_out:
            P.add("pool", lambda e: e.dma_start(out=dbg_out[name], in_=ap_sb), reads=reads, dma=True)

    cst = bf("const")

    def tri_make(dst, base, cm, step, val):
        P.add("pool", lambda e: e.memset(TSCR, val), writes=[bf("OB")])
        P.add("pool", lambda e: e.affine_select(out=TSCR, in_=TSCR, pattern=[[step, 128]], compare_op=ALU.is_ge,
                                                fill=0.0, base=base, channel_multiplier=cm), reads=[bf("OB")], writes=[bf("OB")])
        P.add("pool", lambda e: e.tensor_copy(out=dst[:], in_=TSCR), reads=[bf("OB")], writes=[cst])

    NEG = -1.0 / 16.0
    P.add("pool", lambda e: e.memset(IDF[:], 1.0), writes=[cst])
    P.add("pool", lambda e: e.affine_select(out=IDF[:], in_=IDF[:], pattern=[[-1, 128]], compare_op=ALU.is_equal,
                                            fill=0.0, base=0, channel_multiplier=1), reads=[cst], writes=[cst])
    P.add("pool", lambda e: e.tensor_copy(out=ID[:], in_=IDF[:]), reads=[cst], writes=[cst])
    tri_make(TRIb["f"], 0, -1, 1, NEG)
    tri_make(TRIb["b"], 0, 1, -1, NEG)
    tri_make(TRISb["f"], -1, 1, -1, NEG)
    tri_make(TRISb["b"], -1, -1, 1, NEG)
    tri_make(MASK["f"], 0, -1, 1, 1.0)
    tri_make(MASK["b"], 0, 1, -1, 1.0)
    P.add("pool", lambda e: e.memset(NHALF[:], -0.5), writes=[cst])
    P.add("pool", lambda e: e.memset(ONESR[:], 1.0), writes=[cst])
    P.add("pool", lambda e: e.memset(SELB[:], 1.0), writes=[cst])
    P.add("pool", lambda e: e.affine_select(out=SELB[:], in_=SELB[:], pattern=[[0, 128]], compare_op=ALU.is_equal,
                                            fill=0.0, base=0, channel_multiplier=1), reads=[cst], writes=[cst])
    P.add("pool", lambda e: e.tensor_copy(out=SELC[:], in_=IDF[0:2, 0:2]), reads=[cst], writes=[cst])
    P.add("pool", lambda e: e.memset(AFAB[:], 1.0), writes=[bf("AFAB")])
    for d in "fb":
        P.add("pool", lambda e, d=d: e.memset(WAL[d][:], 0.0), writes=[bf("WAL")])
    P.add("sp", lambda e: e.dma_start(out=C2[:], in_=c2.rearrange("(kc p) j -> p kc j", p=128)), writes=[bf("C2")], dma=True)
    P.add("sp", lambda e: e.dma_start(out=VFM[:], in_=vecs_fm), writes=[bf("VFM")], dma=True)
    P.add("sp", lambda e: e.dma_start(out=FNG[:], in_=rows[0:1, :].partition_broadcast(128)), writes=[cst], dma=True)
    P.add("sp", lambda e: e.dma_start(out=LNG[:], in_=rows[1:2, :].partition_broadcast(128)), writes=[cst], dma=True)
    P.add("sp", lambda e: e.dma_start(out=LNB[:], in_=rows[2:3, :].partition_broadcast(128)), writes=[cst], dma=True)
    P.add("sp", lambda e: e.dma_start(out=WSPF, in_=w_spatial.rearrange("g p q -> p g q")), writes=[bf("GV0_0"), bf("GV0_1")], dma=True)
    P.add("pool", lambda e: e.dma_start(out=BSROW[:], in_=rows[3:4, :]), writes=[cst], dma=True)
    P.add("pool", lambda e: e.dma_start(out=WAL["f"][0:16, :], in_=walpha[0, 0:16, :]), writes=[bf("WAL")], dma=True)
    P.add("pool", lambda e: e.dma_start(out=WAL["f"][32:33, :], in_=walpha[0, 16:17, :]), writes=[bf("WAL")], dma=True)
    P.add("pool", lambda e: e.dma_start(out=WAL["b"][16:32, :], in_=walpha[1, 0:16, :]), writes=[bf("WAL")], dma=True)
    P.add("pool", lambda e: e.dma_start(out=WAL["b"][32:33, :], in_=walpha[1, 16:17, :]), writes=[bf("WAL")], dma=True)
    P.add("pool", lambda e: e.dma_start(out=WAF[:], in_=w_in[:, 5120:5152].rearrange("(kc p) c -> p kc c", p=128)),
          writes=[bf("WAF")], dma=True)

    def cast_k1024(s, W, col0):
        src = W[:, col0:col0 + 512].rearrange("(kc p) c -> p kc c", p=128)
        dst = wsc[s].rearrange("p (kc c) -> p kc c", c=512)
        P.add("pool", lambda e: e.dma_start(out=dst, in_=src), writes=[bWSC[s]], dma=True)

    def cast_ffi(s, j):
        dst = wsc[s].rearrange("p (kc c) -> p kc c", c=512)
        for part, c0 in ((0, j * 256), (1, DFF + j * 256)):
            src = w_fi[:, c0:c0 + 256].rearrange("(kc p) c -> p kc c", p=128)
            d2 = dst[:, :, part * 256:(part + 1) * 256]
            P.add("pool", lambda e, src=src, d2=d2: e.dma_start(out=d2, in_=src), writes=[bWSC[s]], dma=True)

    def cast_ffo(s, h, pc):
        kc0 = 8 * pc
        nk = 8 if pc < 2 else 6
        src = w_fo[kc0 * 128:(kc0 + nk) * 128, h * 512:(h + 1) * 512].rearrange("(kc p) c -> p kc c", p=128)
        dst = wsc[s][:, 0:nk * 512].rearrange("p (kc c) -> p kc c", c=512)
        P.add("pool", lambda e: e.dma_start(out=dst, in_=src), writes=[bWSC[s]], dma=True)

    SL = dict(u=(0, 1), v=(2, 3), q=(4,), k=(5,), vv=(6, 7), r=(8, 9), ga=(10, 11), gb=(12, 13),
              ba=(14, 15), bb=(16, 17), wo=(18, 19))
    COL = dict(u=0, v=1024, q=2048, k=2560, vv=3072, r=4096, ga=5152, gb=6176)

    def cast_group(names):
        for n in names:
            for i, s in enumerate(SL[n]):
                if n in COL:
                    cast_k1024(s, w_in, COL[n] + 512 * i)
                else:
                    cast_k1024(s, dict(ba=w_ba, bb=w_bb, wo=w_out)[n], 512 * i)

    cast_group(["k", "vv", "q"])

    P.add("act", lambda e: e.activation(out=SC[:], in_=C2[:], func=AF.Silu), reads=[bf("C2")], writes=[bf("SC")])
    bada_b = []
    MODB = [bf("MOD")] + [bf("%s%d" % (p_, i_)) for p_ in ("G", "E", "EI") for i_ in range(4)]
    for i in range(12):
        r = next_ring()
        src = w_ada[:, i * 512:(i + 1) * 512].rearrange("(kc p) c -> p kc c", p=128)
        P.add("pool", lambda e, src=src, r=r: e.dma_start(out=WS[r][:], in_=src), writes=[bWS[r]], dma=True)
        bb_ = bf("O")
        P.add("sp", lambda e, i=i: e.dma_start(out=BADA[:, i % 2, :], in_=b_ada[0:1, i * 512:(i + 1) * 512].partition_broadcast(2)),
              writes=[bb_], dma=True)
        pj, bpj = next_pj()
        for kc in range(8):
            P.add("pe", lambda e, kc=kc, r=r, pj=pj: e.matmul(pj[0:2, :], lhsT=SC[:, kc, :], rhs=WS[r][:, kc, :],
                                                            start=(kc == 0), stop=(kc == 7)),
                  reads=[bf("SC"), bWS[r]], writes=[bpj])
        P.add("dve", lambda e, i=i, pj=pj: e.tensor_tensor(out=MOD[:, i * 512:(i + 1) * 512], in0=pj[0:2, :],
                                                         in1=BADA[:, i % 2, :], op=ALU.add),
              reads=[bpj, bb_], writes=MODB)
    pending_casts = []
    for n in ["r", "u", "v", "ga", "gb", "ba", "bb", "wo"]:
        pending_casts.append(lambda n=n: cast_group([n]))
    for j in range(11):
        pending_casts.append(lambda j=j: cast_ffi(20 + j, j))
    for h in range(2):
        for pc in range(3):
            pending_casts.append(lambda h=h, pc=pc: cast_ffo(31 + h * 3 + pc, h, pc))

    def issue_casts(n):
        for _ in range(n):
            if pending_casts:
                pending_casts.pop(0)()
    tap("mod", MOD, MODB)
    pj, bpj = next_pj()
    spec = [(0, 0), (1, 0), (0, 1), (1, 1), (3, 0), (4, 0)]
    for vi, (blk, row) in enumerate(spec):
        for kc in range(8):
            c0 = blk * D + kc * 128
            P.add("pe", lambda e, vi=vi, kc=kc, c0=c0, row=row, pj=pj: e.matmul(
                pj[:, vi * 8 + kc:vi * 8 + kc + 1], lhsT=MOD[0:2, c0:c0 + 128], rhs=SELC[0:2, row:row + 1],
                start=True, stop=True), reads=MODB + [cst], writes=[bpj])
    P.add("dve", lambda e, pj=pj: e.tensor_copy(out=MODF[:].rearrange("p a b -> p (a b)"), in_=pj[:, 0:48]),
          reads=[bpj], writes=[bf("MODF")])
    for (dst, sci, gi) in ((S1, 1, 0), (S1C, 3, 0), (S2, 5, 1)):
        P.add("dve", lambda e, dst=dst, sci=sci, gi=gi: e.scalar_tensor_tensor(
            out=dst[:], in0=MODF[:, sci, :], scalar=1.0, in1=VFM[:, gi, :], op0=ALU.add, op1=ALU.mult),
            reads=[bf("MODF"), bf("VFM")], writes=[bf("S")])
    SH1 = MODF[:, 0, :]; SH1C = MODF[:, 2, :]; SH2 = MODF[:, 4, :]
    for (dst, blk, scl) in ((G1H, 2, 0.5), (G2B, 5, 1.0)):
        for hf in range(2):
            pj, bpj = next_pj()
            c0 = blk * D + hf * 512
            P.add("pe", lambda e, c0=c0, pj=pj: e.matmul(pj[:], lhsT=SELB[0:2, :], rhs=MOD[0:2, c0:c0 + 512],
                                                       start=True, stop=True), reads=MODB + [cst], writes=[bpj])
            P.add("act", lambda e, dst=dst, hf=hf, scl=scl, pj=pj: e.activation(
                out=dst[:, hf * 512:(hf + 1) * 512], in_=pj[:], func=AF.Copy, scale=scl), reads=[bpj], writes=[cst])
    P.add("dve", lambda e: e.tensor_copy(out=UT[:, :, 0:128], in_=WSPF), reads=[bf("GV0_0"), bf("GV0_1")], writes=[bf("UT%d" % i_) for i_ in range(8)])
    for g in range(8):
        pt, bpt = next_ptr()
        P.add("pe", lambda e, g=g, pt=pt: e.transpose(out=pt[:, 0:128], in_=UT[:, g, 0:128], identity=ID[:]),
              reads=[bf("UT%d" % g), cst], writes=[bpt])
        P.add("dve", lambda e, g=g, pt=pt: e.tensor_copy(out=WSPT[:, g, :], in_=pt[:, 0:128]), reads=[bpt], writes=[cst])

    def bl(prefix, n):
        return [bf("%s%d" % (prefix, i)) for i in range(n)]

    GALIAS = bl("G", 4) + bl("E", 4) + bl("EI", 4)

    def load_slot(s, ncols=4096):
        r = next_ring()
        P.add("sp", lambda e: e.dma_start(out=WS[r][:].rearrange("p a b -> p (a b)")[:, 0:ncols], in_=wsc[s][:, 0:ncols]),
              reads=[bWSC[s]], writes=[bWS[r]], dma=True)
        return WS[r], bWS[r]

    def front(src_ap, ntile, Sv, SHv, tagsrc):
        for t in range(ntile):
            P.add("sp", lambda e, t=t: e.dma_start(out=X[:, t, :], in_=src_ap[t * 128:(t + 1) * 128, :]),
                  writes=[bf("X%d" % t)], dma=True)
        norm_to_HT(ntile, Sv, SHv)

    def norm_to_HT(ntile, Sv, SHv):
        for t in range(ntile):
            P.add("act", lambda e, t=t: e.activation(out=XN[:, t, :], in_=X[:, t, :], func=AF.Square, accum_out=STAT[:, t:t + 1]),
                  reads=[bf("X%d" % t)], writes=[bf("STa%d" % t), bf("XN%d" % t)])
        P.add("dve", lambda e: e.tensor_scalar(out=STAT[:, 4:4 + ntile], in0=STAT[:, 0:ntile], scalar1=1.0 / D, scalar2=1e-6,
                                               op0=ALU.mult, op1=ALU.add), reads=bl("STa", ntile), writes=[bf("STb")] + bl("STb", 4))
        P.add("pool", lambda e: e.tensor_tensor(out=STAT[:, 8:8 + ntile], in0=STAT[:, 4:4 + ntile], in1=NHALF[:, 0:ntile], op=ALU.pow),
              reads=[bf("STb"), cst], writes=[bf("STc")] + bl("STc", 4))
        for t in range(ntile):
            P.add("dve", lambda e, t=t: e.tensor_scalar(out=XN[:, t, :], in0=X[:, t, :], scalar1=STAT[:, 8 + t:9 + t], scalar2=None,
                                                        op0=ALU.mult), reads=[bf("X%d" % t), bf("STc")], writes=[bf("XN%d" % t)])
        xn_to_fm(ntile, lambda e, kc, pt, n: e.tensor_scalar(out=HT[:, kc, 0:n], in0=pt[:, 0:n], scalar1=Sv[:, kc:kc + 1],
                                                             scalar2=SHv[:, kc:kc + 1], op0=ALU.mult, op1=ALU.add),
                 lambda kc: [bf("S"), bf("MODF")], lambda kc: [bf("HT%d" % kc)],
                 act_evac=lambda e, kc, pt, n: e.activation(out=HT[:, kc, 0:n], in_=pt[:, 0:n], func=AF.Identity,
                                                            scale=Sv[:, kc:kc + 1], bias=SHv[:, kc:kc + 1]))

    def xn_to_fm(ntile, evac, ereads, ewrites, act_evac=None):
        n = ntile * 128
        for kc in range(8):
            pt, bpt = next_ptr()
            for t in range(ntile):
                P.add("pe", lambda e, t=t, kc=kc, pt=pt: e.transpose(out=pt[:, t * 128:(t + 1) * 128], in_=XN[:, t, kc * 128:(kc + 1) * 128],
                                                                     identity=ID[:]), reads=[bf("XN%d" % t), cst], writes=[bpt])
            if act_evac is not None and kc % 2 == 1:
                P.add("act", lambda e, kc=kc, pt=pt: act_evac(e, kc, pt, n), reads=[bpt] + ereads(kc), writes=ewrites(kc))
            else:
                P.add("dve", lambda e, kc=kc, pt=pt: evac(e, kc, pt, n), reads=[bpt] + ereads(kc), writes=ewrites(kc))

    def proj_fm(ws, bws, c, n, evac_eng, evac, ereads, ewrites, M=128, lhs_cols=None, src=None, bsrc="HT"):
        src = HT if src is None else src
        pj, bpj = next_pj()
        for kc in range(8):
            lhs = ws[:, kc, c * 128:c * 128 + M] if lhs_cols is None else ws[:, kc, lhs_cols[0]:lhs_cols[1]]
            P.add("pe", lambda e, kc=kc, lhs=lhs, pj=pj: e.matmul(pj[0:M, 0:n], lhsT=lhs, rhs=src[:, kc, 0:n],
                                                                start=(kc == 0), stop=(kc == 7)),
                  reads=[bws, bf("%s%d" % (bsrc, kc))], writes=[bpj])
        P.add(evac_eng, lambda e, pj=pj: evac(e, pj), reads=[bpj] + ereads, writes=ewrites)

    def proj_tm(ws, bws, t, evac_eng, evac, ereads, ewrites, src=None, bsrc=None):
        src = HT if src is None else src
        bsrc = (lambda kc: bf("HT%d" % kc)) if bsrc is None else bsrc
        pj, bpj = next_pj()
        for kc in range(8):
            P.add("pe", lambda e, kc=kc, pj=pj: e.matmul(pj[:], lhsT=src[:, kc, t * 128:(t + 1) * 128], rhs=ws[:, kc, :],
                                                       start=(kc == 0), stop=(kc == 7)),
                  reads=[bws, bsrc(kc)], writes=[bpj])
        P.add(evac_eng, lambda e, pj=pj: evac(e, pj), reads=[bpj] + ereads, writes=ewrites)

    def gates(ntile, d, need_o):
        n = ntile * 128
        ep = elpar[0]
        proj_fm(WAF, bf("WAF"), 0, n, "dve", lambda e, pj: e.tensor_copy(out=AFAB[0:32, 0:n], in_=pj[0:32, 0:n]),
                [], [bf("AFAB")], M=32, lhs_cols=(0, 32))
        gps = []
        for t in range(ntile):
            pj, bpj = next_pj()
            P.add("pe", lambda e, t=t, pj=pj: e.matmul(pj[:], lhsT=AFAB[0:33, t * 128:(t + 1) * 128], rhs=WAL[d][0:33, :],
                                                     start=True, stop=True), reads=[bf("AFAB"), bf("WAL")], writes=[bpj])
            P.add("act", lambda e, t=t, pj=pj: e.activation(out=G_[:, t, :], in_=pj[:], func=AF.Exp, scale=-1.0),
                  reads=[bpj], writes=[bf("G%d" % t)])
            P.add("act", lambda e, t=t: e.activation(out=G_[:, t, :], in_=G_[:, t, :], func=AF.Ln, bias=1.0),
                  reads=[bf("G%d" % t)], writes=[bf("G%d" % t)])
            gp = t
            gps.append(gp)
            bgh = bf("GHL%d" % gp)
            P.add("dve", lambda e, t=t, gp=gp: e.tensor_copy(out=GHL[:, gp, 0, :], in_=G_[:, t, :]), reads=[bf("G%d" % t)], writes=[bgh])
            P.add("dve", lambda e, t=t, gp=gp: e.tensor_tensor(out=GHL[:, gp, 1, :], in0=G_[:, t, :], in1=GHL[:, gp, 0, :], op=ALU.subtract),
                  reads=[bf("G%d" % t), bgh], writes=[bgh])
        for t in range(ntile):
            gp = gps[t]
            bgh = bf("GHL%d" % gp)
            pj, bpj = next_pj()
            for h in range(4):
                for hl in range(2):
                    P.add("pe", lambda e, h=h, hl=hl, gp=gp, pj=pj: e.matmul(pj[:, h * 128:(h + 1) * 128], lhsT=GHL[:, gp, hl, h * 128:(h + 1) * 128],
                                                                          rhs=TRIb[d][:], start=(hl == 0), stop=(hl == 1)),
                          reads=[bgh, cst], writes=[bpj])
            pj3 = pj[:].rearrange("p (h c) -> p h c", c=128)
            if need_o:
                P.add("act", lambda e, t=t, pj3=pj3: e.activation(out=E_[:, :, t * 128:(t + 1) * 128], in_=pj3, func=AF.Exp),
                      reads=[bpj], writes=[bf("E%d" % t)])
                P.add("act", lambda e, t=t, pj3=pj3: e.activation(out=EI_[:, :, t * 128:(t + 1) * 128], in_=pj3, func=AF.Exp, scale=-1.0),
                      reads=[bpj], writes=[bf("EI%d" % t)])
            li = 127 if d == "f" else 0
            P.add("act", lambda e, t=t, pj3=pj3, li=li, ep=ep: e.activation(out=ELAST2[:, ep, t, :], in_=pj3[:, :, li], func=AF.Exp),
                  reads=[bpj], writes=[bf("EL%d_%d" % (ep, t))])
            pj, bpj = next_pj()
            for hl in range(2):
                P.add("pe", lambda e, hl=hl, gp=gp, pj=pj: e.matmul(pj[:], lhsT=TRISb[d][:], rhs=GHL[:, gp, hl, :], start=(hl == 0), stop=(hl == 1)),
                      reads=[bgh, cst], writes=[bpj])
            P.add("act", lambda e, t=t, pj=pj: e.activation(out=ER_[:, t, :], in_=pj[:], func=AF.Exp), reads=[bpj], writes=[bf("ER%d" % t)])

    def gla_mults(ntile):
        n = ntile * 128
        for h in range(4):
            P.add("dve", lambda e, h=h: e.tensor_tensor(out=QD[:, h, 0:n], in0=QD[:, h, 0:n], in1=E_[:, h, 0:n], op=ALU.mult),
                  reads=[bf("QD%d" % h)] + bl("E", ntile), writes=[bf("QD%d" % h)])
        for h in range(4):
            P.add("dve", lambda e, h=h: e.tensor_tensor(out=KI[:, h, 0:n], in0=KI[:, h, 0:n], in1=EI_[:, h, 0:n], op=ALU.mult),
                  reads=[bf("KI%d" % h)] + bl("EI", ntile), writes=[bf("KI%d" % h)])
        for t in range(ntile):
            P.add("dve", lambda e, t=t: e.tensor_tensor(out=KST[:, t, :], in0=KST[:, t, :], in1=ER_[:, t, :], op=ALU.mult),
                  reads=[bf("KST%d" % t), bf("ER%d" % t)], writes=[bf("KST%d" % t)])

    PRJ_PARTS = [(0, 2048, lambda: bl("QD", 4)), (2048, 4096, lambda: bl("KI", 4)), (4096, 6144, lambda: bl("KST", 4)),
                 (6144, 10240, lambda: [bf("VV%d_%d" % (t, hf)) for t in range(4) for hf in range(2)])]

    def gla_inputs(ntile, need_o, mode=None, g=None, phase=None):
        n = ntile * 128
        if mode == "B":
            for pi, (c0, c1, bufs) in enumerate(PRJ_PARTS):
                P.add("sp", lambda e, c0=c0, c1=c1: e.dma_start(out=GLA[:, c0:c1], in_=prj[g][:, c0:c1]),
                      reads=[bPRJ[g][pi]], writes=bufs(), dma=True)
            gla_mults(ntile)
            return
        raw = mode == "A"
        if raw and phase == "fin":
            for pi, (c0, c1, bufs) in enumerate(PRJ_PARTS):
                P.add("sp", lambda e, c0=c0, c1=c1: e.dma_start(out=prj[g][:, c0:c1], in_=GLA[:, c0:c1]),
                      reads=bufs(), writes=[bPRJ[g][pi]], dma=True)
            gla_mults(ntile)
            return
        if need_o:
            ws, bws = load_slot(SL["q"][0])
            for h in range(4):
                if raw:
                    ev = lambda e, pj, h=h: e.tensor_scalar(out=QD[:, h, 0:n], in0=pj[:, 0:n], scalar1=128.0 ** -0.5, scalar2=None, op0=ALU.mult)
                    rd = []
                else:
                    ev = lambda e, pj, h=h: e.scalar_tensor_tensor(out=QD[:, h, 0:n], in0=pj[:, 0:n], scalar=128.0 ** -0.5,
                                                                   in1=E_[:, h, 0:n], op0=ALU.mult, op1=ALU.mult)
                    rd = bl("E", ntile)
                proj_fm(ws, bws, h, n, "dve", ev, rd, [bf("QD%d" % h)])
        ws, bws = load_slot(SL["k"][0])
        if need_o:
            for h in range(4):
                if raw:
                    proj_fm(ws, bws, h, n, "act", lambda e, pj, h=h: e.activation(out=KI[:, h, 0:n], in_=pj[:, 0:n], func=AF.Copy),
                            [], [bf("KI%d" % h)])
                else:
                    proj_fm(ws, bws, h, n, "dve",
                            lambda e, pj, h=h: e.tensor_tensor(out=KI[:, h, 0:n], in0=pj[:, 0:n], in1=EI_[:, h, 0:n], op=ALU.mult),
                            bl("EI", ntile), [bf("KI%d" % h)])
        for t in range(ntile):
            if raw:
                proj_tm(ws, bws, t, "dve", lambda e, pj, t=t: e.tensor_copy(out=KST[:, t, :], in_=pj[:]), [], [bf("KST%d" % t)])
            else:
                proj_tm(ws, bws, t, "dve", lambda e, pj, t=t: e.tensor_tensor(out=KST[:, t, :], in0=pj[:], in1=ER_[:, t, :], op=ALU.mult),
                        [bf("ER%d" % t)], [bf("KST%d" % t)])
        for hf in range(2):
            ws, bws = load_slot(SL["vv"][hf])
            for t in range(ntile):
                proj_tm(ws, bws, t, "act", lambda e, pj, t=t, hf=hf: e.activation(out=VV[:, t, hf * 512:(hf + 1) * 512], in_=pj[:], func=AF.Copy),
                        [], [bf("VV%d_%d" % (t, hf))])
        if raw and phase is None:
            gla_inputs(ntile, need_o, mode, g, phase="fin")

    def gla_tile(t, d, need_o, o_done, ep):
        vvb = [bf("VV%d_0" % t), bf("VV%d_1" % t)]
        cur = spar[0]
        nxt = 1 - cur
        spar[0] = nxt
        if need_o:
            pj, bpj = next_pj()
            for h in range(4):
                P.add("pe", lambda e, h=h, pj=pj: e.matmul(pj[:, h * 128:(h + 1) * 128], lhsT=KI[:, h, t * 128:(t + 1) * 128],
                                                         rhs=QD[:, h, t * 128:(t + 1) * 128], start=True, stop=True),
                      reads=[bf("KI%d" % h), bf("QD%d" % h)], writes=[bpj])
            P.add("dve", lambda e, pj=pj: e.tensor_tensor(out=ATT[:], in0=pj[:].rearrange("p (h c) -> p h c", c=128),
                                                        in1=MASK[d][:].unsqueeze(1).broadcast_to([128, 4, 128]), op=ALU.mult),
                  reads=[bpj, cst], writes=[bf("ATT")])
        for h in range(4):
            pd = PO[2 + h // 2][:, (h % 2) * 256:(h % 2) * 256 + 256]
            P.add("pe", lambda e, h=h, pd=pd: e.matmul(pd, lhsT=KST[:, t, h * 128:(h + 1) * 128], rhs=VV[:, t, h * 256:(h + 1) * 256],
                                                     start=True, stop=True), reads=[bf("KST%d" % t), vvb[h // 2]], writes=[bPO[2 + h // 2]])
        for h in range(4):
            pd = PO[2 + h // 2][:, (h % 2) * 256:(h % 2) * 256 + 256]
            P.add("dve", lambda e, h=h, pd=pd: e.scalar_tensor_tensor(out=S32[:, h, :], in0=S32[:, h, :], scalar=ELAST2[:, ep, t, h:h + 1],
                                                                    in1=pd, op0=ALU.mult, op1=ALU.add),
                  reads=[bPO[2 + h // 2], bf("EL%d_%d" % (ep, t)), bf("S32")], writes=[bf("S32")])
        P.add("pool", lambda e: e.tensor_copy(out=SBF2[:, nxt, :, :], in_=S32[:]), reads=[bf("S32")], writes=[bf("SBFp%d" % nxt)])
        if need_o:
            for h in range(4):
                po = PO[h // 2][:, (h % 2) * 256:(h % 2) * 256 + 256]
                P.add("pe", lambda e, h=h, po=po: e.matmul(po, lhsT=ATT[:, h, :], rhs=VV[:, t, h * 256:(h + 1) * 256],
                                                         start=True, stop=False), reads=[bf("ATT"), vvb[h // 2]], writes=[bPO[h // 2]])
                P.add("pe", lambda e, h=h, po=po: e.matmul(po, lhsT=QD[:, h, t * 128:(t + 1) * 128], rhs=SBF2[:, cur, h, :],
                                                         start=False, stop=True), reads=[bf("QD%d" % h), bf("SBFp%d" % cur)], writes=[bPO[h // 2]])
            o_done(t)

    NG = 8
    ngA = 1 if stop_after == "A1" else NG
    seqA = [(ctx, 2, S1C, SH1C, False, 0)] + [(x[g * 512:(g + 1) * 512, :], 4, S1, SH1, True, g * 512) for g in reversed(range(NG - ngA, NG))]

    def inputsA(item):
        if item[4]:
            gla_inputs(item[1], True, mode="A", g=item[5] // 512)
        else:
            gla_inputs(item[1], False)


    def xloadA(item):
        src_ap, ntile = item[0], item[1]
        for t in range(ntile):
            P.add("sp", lambda e, t=t: e.dma_start(out=X[:, t, :], in_=src_ap[t * 128:(t + 1) * 128, :]),
                  writes=[bf("X%d" % t)], dma=True)

    def preA(item):
        src_ap, ntile, Sv, SHv, need_o, row0 = item
        norm_to_HT(ntile, Sv, SHv)
        gates(ntile, "b", need_o)

    def tilesA(item, ep):
        src_ap, ntile, Sv, SHv, need_o, row0 = item

        def o_done(t):
            for i in range(2):
                P.add("act", lambda e, i=i: e.activation(out=O[:, i * 512:(i + 1) * 512], in_=PO[i][:], func=AF.Copy),
                      reads=[bPO[i]], writes=[bf("O")])
            r0 = row0 + t * 128
            P.add("pool", lambda e: e.dma_start(out=obs[r0:r0 + 128, :], in_=O[:]), reads=[bf("O")], writes=[bOBS[r0 // 128]], dma=True)

        for t in reversed(range(ntile)):
            gla_tile(t, "b", need_o, o_done, ep)

    P.add("pool", lambda e: e.memset(S32[:], 0.0), writes=[bf("S32")])
    P.add("pool", lambda e: e.memset(SBF2[:], 0.0), writes=[bf("SBFp0"), bf("SBFp1")])
    elpar[0] = 0
    xloadA(seqA[0])
    preA(seqA[0])
    if len(seqA) > 1:
        xloadA(seqA[1])
    inputsA(seqA[0])
    issue_casts(4)
    for i in range(1, len(seqA)):
        item = seqA[i]
        pj_pool[0] = [3]
        P.capture(); tilesA(seqA[i - 1], (i - 1) % 2); sT = P.end_capture()
        pj_pool[0] = [0, 1, 2]
        P.capture(); norm_to_HT(item[1], item[2], item[3]); sN = P.end_capture()
        P.commit(sT, sN)
        if i + 1 < len(seqA):
            xloadA(seqA[i + 1])
        elpar[0] = i % 2
        pj_pool[0] = [0, 1]
        P.capture(); gates(item[1], "b", True); sG = P.end_capture()
        pj_pool[0] = [2, 3]
        P.capture(); gla_inputs(item[1], True, mode="A", g=item[5] // 512, phase="raw"); sR = P.end_capture()
        P.commit(sG, sR)
        pj_pool[0] = [0, 1, 2, 3]
        gla_inputs(item[1], True, mode="A", g=item[5] // 512, phase="fin")
        issue_casts(4)
    tilesA(seqA[-1], (len(seqA) - 1) % 2)
    issue_casts(100)
    if stop_after == "A1":
        tap("ob", O[:], [bf("O")])
        P.emit()
        return nc, P

    def sweepB_ctx():
        P.add("pool", lambda e: e.memset(S32[:], 0.0), writes=[bf("S32")])
        P.add("pool", lambda e: e.memset(SBF2[:], 0.0), writes=[bf("SBFp0"), bf("SBFp1")])
        elpar[0] = 0
        front(ctx, 2, S1C, SH1C, None)
        gates(2, "f", False)
        gla_inputs(2, False)
        for t in range(2):
            gla_tile(t, "f", False, None, 0)

    def load_X(g):
        for t in range(4):
            P.add("sp", lambda e, t=t: e.dma_start(out=X[:, t, :], in_=x[g * 512 + t * 128:g * 512 + (t + 1) * 128, :]),
                  writes=[bf("X%d" % t)], dma=True)

    def front_early(g, part):
        if part == 2:
            xn_to_fm(4, lambda e, kc, pt, n: e.tensor_scalar(out=HT[:, kc, 0:n], in0=pt[:, 0:n], scalar1=S1[:, kc:kc + 1],
                                                             scalar2=SH1[:, kc:kc + 1], op0=ALU.mult, op1=ALU.add),
                     lambda kc: [bf("S"), bf("MODF")], lambda kc: [bf("HT%d" % kc)],
                     act_evac=lambda e, kc, pt, n: e.activation(out=HT[:, kc, 0:n], in_=pt[:, 0:n], func=AF.Identity,
                                                                scale=S1[:, kc:kc + 1], bias=SH1[:, kc:kc + 1]))
            return
        stg = [(O, bf("O")), (OB, bf("OB"))]
        for t in range(4):
            S_, bS = stg[t % 2]
            P.add("sp", lambda e, t=t, S_=S_: e.dma_start(out=S_[:], in_=x[g * 512 + t * 128:g * 512 + (t + 1) * 128, :]),
                  writes=[bS], dma=True)
            P.add("act", lambda e, t=t, S_=S_: e.activation(out=XN[:, t, :], in_=S_[:], func=AF.Square, accum_out=STAT[:, t:t + 1]),
                  reads=[bS], writes=[bf("STa%d" % t), bf("XN%d" % t)])
            P.add("dve", lambda e, t=t: e.tensor_scalar(out=STAT[:, 4 + t:5 + t], in0=STAT[:, t:t + 1], scalar1=1.0 / D, scalar2=1e-6,
                                                        op0=ALU.mult, op1=ALU.add), reads=[bf("STa%d" % t)], writes=[bf("STb%d" % t)])
            P.add("pool", lambda e, t=t: e.tensor_tensor(out=STAT[:, 8 + t:9 + t], in0=STAT[:, 4 + t:5 + t], in1=NHALF[:, 0:1], op=ALU.pow),
                  reads=[bf("STb%d" % t), cst], writes=[bf("STc%d" % t)])
            P.add("dve", lambda e, t=t, S_=S_: e.tensor_scalar(out=XN[:, t, :], in0=S_[:], scalar1=STAT[:, 8 + t:9 + t], scalar2=None,
                                                             op0=ALU.mult), reads=[bS, bf("STc%d" % t)], writes=[bf("XN%d" % t)])

    def sweepB_group(g, ng):
        row0 = g * 512
        if g == 0:
            front_early(0, 1)
            front_early(0, 2)
            load_X(0)
        elpar[0] = 0
        gates(4, "f", True)
        pj_pool[0] = [0, 1]
        ws_pool[0] = [0, 1]
        P.capture()
        gla_inputs(4, True, mode="B", g=g)

        def o_done(t):
            r0 = row0 + t * 128
            P.add("pool", lambda e: e.dma_start(out=OB[:], in_=obs[r0:r0 + 128, :]), reads=[bOBS[r0 // 128]], writes=[bf("OB")], dma=True)
            for i in range(2):
                P.add("dve", lambda e, i=i: e.tensor_tensor(out=O[:, i * 512:(i + 1) * 512], in0=PO[i][:], in1=OB[:, i * 512:(i + 1) * 512],
                                                          op=ALU.add), reads=[bPO[i], bf("OB")], writes=[bf("O")])
            for h in range(4):
                P.add("act", lambda e, h=h: e.activation(out=XN[:, t, h * 256:(h + 1) * 256], in_=O[:, h * 256:(h + 1) * 256], func=AF.Square,
                                                         accum_out=STAT[:, 12 + h:13 + h]), reads=[bf("O")], writes=[bf("STATO"), bf("XN%d" % t)])
            P.add("dve", lambda e: e.tensor_scalar(out=STAT[:, 12:16], in0=STAT[:, 12:16], scalar1=1.0 / 256, scalar2=1e-6,
                                                   op0=ALU.mult, op1=ALU.add), reads=[bf("STATO")], writes=[bf("STATO")])
            P.add("pool", lambda e: e.tensor_tensor(out=STAT[:, 12:16], in0=STAT[:, 12:16], in1=NHALF[:, 0:4], op=ALU.pow),
                  reads=[bf("STATO"), cst], writes=[bf("STATO")])
            for h in range(4):
                P.add("dve", lambda e, h=h: e.tensor_scalar(out=XN[:, t, h * 256:(h + 1) * 256], in0=O[:, h * 256:(h + 1) * 256],
                                                            scalar1=STAT[:, 12 + h:13 + h], scalar2=None, op0=ALU.mult),
                      reads=[bf("O"), bf("STATO")], writes=[bf("XN%d" % t)])

        for t in range(4):
            gla_tile(t, "f", True, o_done, 0)
        s1 = P.end_capture()
        pj_pool[0] = [2, 3]
        ws_pool[0] = [0, 1, 2, 3]
        P.capture()
        for hf in range(2):
            ws, bws = load_slot(SL["r"][hf])
            for c in range(4):
                proj_fm(ws, bws, c, 512, "act", lambda e, pj, c=c, hf=hf: e.activation(out=RT[:, hf * 4 + c, :], in_=pj[:], func=AF.Silu),
                        [], [bf("RT%d" % (hf * 4 + c))])
        for hf in range(2):
            ws, bws = load_slot(SL["u"][hf])
            for c in range(4):
                proj_fm(ws, bws, c, 512, "act", lambda e, pj, c=c, hf=hf: e.activation(out=UT[:, hf * 4 + c, :], in_=pj[:], func=AF.Gelu),
                        [], [bf("UT%d" % (hf * 4 + c))])
        wsv = [load_slot(SL["v"][hf]) for hf in range(2)]
        def vproj(t):
            gq = t % 2
            GVq = GV2[:, gq, :]
            gvb = [bf("GV%d_%d" % (gq, 0)), bf("GV%d_%d" % (gq, 1))]
            bns, bna = bf("BNS%d" % gq), bf("BNA%d" % gq)
            for hf in range(2):
                proj_tm(wsv[hf][0], wsv[hf][1], t, "act",
                        lambda e, pj, hf=hf, GVq=GVq: e.activation(out=GVq[:, hf * 512:(hf + 1) * 512], in_=pj[:], func=AF.Gelu), [], [gvb[hf]])

        def ln_spatial(t):
            gq = t % 2
            GVq = GV2[:, gq, :]
            gvb = [bf("GV%d_%d" % (gq, 0)), bf("GV%d_%d" % (gq, 1))]
            bns, bna = bf("BNS%d" % gq), bf("BNA%d" % gq)
            for hf in range(2):
                P.add("dve", lambda e, hf=hf, GVq=GVq, gq=gq: e.bn_stats(out=BNS[:, gq, hf, :], in_=GVq[:, hf * 512:(hf + 1) * 512]), reads=[gvb[hf]], writes=[bns])
            bmu = bf("BMU%d" % gq)
            P.add("dve", lambda e, gq=gq: e.bn_aggr(out=BNA[:, gq, :], in_=BNS[:, gq, :, :].rearrange("p a b -> p (a b)")), reads=[bns], writes=[bna, bmu])
            P.add("dve", lambda e, gq=gq: e.tensor_scalar(out=BNA[:, gq, 1:2], in0=BNA[:, gq, 1:2], scalar1=1e-6, scalar2=None, op0=ALU.add),
                  reads=[bna], writes=[bna])
            P.add("pool", lambda e, gq=gq: e.tensor_tensor(out=BNA[:, gq, 1:2], in0=BNA[:, gq, 1:2], in1=NHALF[:, 0:1], op=ALU.pow),
                  reads=[bna, cst], writes=[bna])
            P.add("dve", lambda e, GVq=GVq, gq=gq: e.scalar_tensor_tensor(out=GVq, in0=GVq, scalar=BNA[:, gq, 0:1], in1=LNG[:], op0=ALU.subtract, op1=ALU.mult),
                  reads=gvb + [bmu, cst], writes=gvb)
            P.add("dve", lambda e, t=t, GVq=GVq, gq=gq: e.scalar_tensor_tensor(out=VN[:, t, :], in0=GVq, scalar=BNA[:, gq, 1:2], in1=LNB[:], op0=ALU.mult, op1=ALU.add),
                  reads=gvb + [bna, cst], writes=[bf("VN%d" % t)])
            for half in range(2):
                pj, bpj = next_pj()
                P.add("pe", lambda e, half=half, pj=pj: e.matmul(pj[:], lhsT=ONESR[0:1, :], rhs=BSROW[0:1, half * 512:(half + 1) * 512],
                                                               start=True, stop=False, skip_group_check=True),
                      reads=[cst], writes=[bpj])
                for gg in range(4):
                    g8 = half * 4 + gg
                    P.add("pe", lambda e, g8=g8, gg=gg, pj=pj, t=t: e.matmul(pj[:, gg * 128:(gg + 1) * 128], lhsT=VN[:, t, g8 * 128:(g8 + 1) * 128],
                                                                          rhs=WSPT[:, g8, :], start=False, stop=(gg == 3), skip_group_check=True),
                          reads=[bf("VN%d" % t), cst], writes=[bpj])
                utb = [bf("UT%d" % (half * 4 + gg)) for gg in range(4)]
                P.add("dve", lambda e, half=half, pj=pj, t=t: e.tensor_tensor(
                    out=UT[:, half * 4:half * 4 + 4, t * 128:(t + 1) * 128], in0=pj[:].rearrange("p (g c) -> p g c", c=128),
                    in1=UT[:, half * 4:half * 4 + 4, t * 128:(t + 1) * 128], op=ALU.mult), reads=[bpj] + utb, writes=utb)

        vproj(0)
        for t in range(4):
            if t + 1 < 4:
                vproj(t + 1)
            ln_spatial(t)
        s2 = P.end_capture()
        kk = min(len(s2), 2 * len(s1))
        P.commit(s1, s2[:kk])
        P.commit(s2[kk:])
        pj_pool[0] = [0, 1, 2, 3]
        ws_pool[0] = [0, 1, 2, 3]
        def tab(nm, c8):
            if nm == "ga":
                return [bf("QD%d" % c8)] if c8 < 4 else [bf("KI%d" % (c8 - 4))]
            return [bf("KST%d" % c8)] if c8 < 4 else [bf("VV%d_0" % ((c8 - 4) // 2)), bf("VV%d_1" % ((c8 - 4) // 2))]
        for nm, dst in (("ga", TA), ("gb", TB)):
            for hf in range(2):
                ws, bws = load_slot(SL[nm][hf])
                for c in range(4):
                    proj_fm(ws, bws, c, 512, "act",
                            lambda e, pj, c=c, hf=hf, dst=dst: e.activation(out=dst[:, hf * 4 + c, :], in_=pj[:], func=AF.Tanh, scale=0.5),
                            [], tab(nm, hf * 4 + c))
        xn_to_fm(4, lambda e, kc, pt, n: e.scalar_tensor_tensor(out=RT[:, kc, :], in0=pt[:, 0:n], scalar=VFM[:, 2, kc:kc + 1],
                                                                in1=RT[:, kc, :], op0=ALU.mult, op1=ALU.mult),
                 lambda kc: [bf("VFM"), bf("RT%d" % kc)], lambda kc: [bf("RT%d" % kc)])
        wsa = [load_slot(SL["ba"][hf]) for hf in range(2)]
        wsb = [load_slot(SL["bb"][hf]) for hf in range(2)]
        for c8 in range(8):
            hf, c = divmod(c8, 4)
            pja, bpja = next_pj()
            for kc in range(8):
                P.add("pe", lambda e, kc=kc, pja=pja, hf=hf, c=c: e.matmul(pja[:], lhsT=wsa[hf][0][:, kc, c * 128:(c + 1) * 128], rhs=UT[:, kc, :],
                                                                         start=(kc == 0), stop=(kc == 7)), reads=[wsa[hf][1], bf("UT%d" % kc)], writes=[bpja])
            P.add("dve", lambda e, pja=pja, c8=c8: e.scalar_tensor_tensor(out=TMP[:], in0=TA[:, c8, :], scalar=1.0, in1=pja[:], op0=ALU.add, op1=ALU.mult),
                  reads=[bpja] + tab("ga", c8), writes=[bf("TMP")])
            pjb, bpjb = next_pj()
            for kc in range(8):
                P.add("pe", lambda e, kc=kc, pjb=pjb, hf=hf, c=c: e.matmul(pjb[:], lhsT=wsb[hf][0][:, kc, c * 128:(c + 1) * 128], rhs=RT[:, kc, :],
                                                                         start=(kc == 0), stop=(kc == 7)), reads=[wsb[hf][1], bf("RT%d" % kc)], writes=[bpjb])
            P.add("dve", lambda e, pjb=pjb, c8=c8: e.scalar_tensor_tensor(out=GV[:, 0:512], in0=TB[:, c8, :], scalar=1.0, in1=pjb[:], op0=ALU.add, op1=ALU.mult),
                  reads=[bpjb] + tab("gb", c8), writes=[bf("GV0_0")])
            P.add("dve", lambda e, c8=c8: e.tensor_tensor(out=YT[:, c8, :], in0=TMP[:], in1=GV[:, 0:512], op=ALU.add),
                  reads=[bf("TMP"), bf("GV0_0")], writes=[bf("VN%d" % (c8 // 2))])
        wso = [load_slot(SL["wo"][hf]) for hf in range(2)]
        for t in range(4):
            for hf in range(2):
                def ev(e, pj, t=t, hf=hf):
                    return e.tensor_tensor(out=TMP[:], in0=pj[:], in1=G1H[:, hf * 512:(hf + 1) * 512], op=ALU.mult)
                proj_tm(wso[hf][0], wso[hf][1], t, "dve", ev, [cst], [bf("TMP")], src=YT, bsrc=lambda kc: bf("VN%d" % (kc // 2)))
                P.add("dve", lambda e, t=t, hf=hf: e.tensor_tensor(out=X[:, t, hf * 512:(hf + 1) * 512], in0=X[:, t, hf * 512:(hf + 1) * 512],
                                                                 in1=TMP[:], op=ALU.add), reads=[bf("TMP"), bf("X%d" % t)], writes=[bf("X%d" % t)])
        norm_to_HT(4, S2, SH2)
        for j in range(11):
            ws, bws = load_slot(20 + j)
            for sub in range(2):
                pa, bpa = next_pj()
                for kc in range(8):
                    P.add("pe", lambda e, kc=kc, pa=pa, sub=sub, ws=ws: e.matmul(pa[:], lhsT=ws[:, kc, sub * 128:(sub + 1) * 128], rhs=HT[:, kc, :],
                                                                               start=(kc == 0), stop=(kc == 7)), reads=[bws, bf("HT%d" % kc)], writes=[bpa])
                pg, bpg = next_pj()
                for kc in range(8):
                    P.add("pe", lambda e, kc=kc, pg=pg, sub=sub, ws=ws: e.matmul(pg[:], lhsT=ws[:, kc, 256 + sub * 128:256 + (sub + 1) * 128], rhs=HT[:, kc, :],
                                                                               start=(kc == 0), stop=(kc == 7)), reads=[bws, bf("HT%d" % kc)], writes=[bpg])
                P.add("act", lambda e, pg=pg: e.activation(out=TMP[:], in_=pg[:], func=AF.Silu), reads=[bpg], writes=[bf("TMP")])
                P.add("dve", lambda e, pa=pa, j=j, sub=sub: e.tensor_tensor(out=ACTT[:, 2 * j + sub, :], in0=pa[:], in1=TMP[:], op=ALU.mult),
                      reads=[bpa, bf("TMP")], writes=[bf("ACTT%d" % (2 * j + sub))] + GALIAS)
        P.capture()
        for hf in range(2):
            for pc in range(3):
                nk = 8 if pc < 2 else 6
                ws, bws = load_slot(31 + hf * 3 + pc, nk * 512)
                for t in range(4):
                    for k in range(nk):
                        kc = pc * 8 + k
                        P.add("pe", lambda e, t=t, k=k, kc=kc, ws=ws: e.matmul(PO[t][:], lhsT=ACTT[:, kc, t * 128:(t + 1) * 128], rhs=ws[:, k, :],
                                                                             start=(kc == 0), stop=(kc == 21)),
                              reads=[bws, bf("ACTT%d" % kc)] + (GALIAS if (hf == 1 and t == 3) else []), writes=[bPO[t]])
            for t in range(4):
                P.add("dve", lambda e, t=t, hf=hf: e.tensor_tensor(out=TMP[:], in0=PO[t][:], in1=G2B[:, hf * 512:(hf + 1) * 512], op=ALU.mult),
                      reads=[bPO[t], cst], writes=[bf("TMP")])
                P.add("dve", lambda e, t=t, hf=hf: e.tensor_tensor(out=X[:, t, hf * 512:(hf + 1) * 512], in0=X[:, t, hf * 512:(hf + 1) * 512],
                                                                 in1=TMP[:], op=ALU.add), reads=[bf("TMP"), bf("X%d" % t)], writes=[bf("X%d" % t)])
        sFO = P.end_capture()
        if g + 1 < ng:
            pj_pool[0] = [0, 1, 2, 3]
            P.capture()
            front_early(g + 1, 1)
            sFE = P.end_capture()
            P.commit(sFO, sFE)
            front_early(g + 1, 2)
        else:
            P.commit(sFO)
        UTJ = UT[:, 0:2, :].rearrange("p a b -> p (a b)")
        for t in range(4):
            P.add("act", lambda e, t=t: e.activation(out=UTJ, in_=X[:, t, :], func=AF.Square, accum_out=STAT[:, t:t + 1]),
                  reads=[bf("X%d" % t)], writes=[bf("STa%d" % t), bf("UT0"), bf("UT1")])
        P.add("dve", lambda e: e.tensor_scalar(out=STAT[:, 4:8], in0=STAT[:, 0:4], scalar1=1.0 / D, scalar2=1e-6, op0=ALU.mult, op1=ALU.add),
              reads=bl("STa", 4), writes=[bf("STb")] + bl("STb", 4))
        P.add("pool", lambda e: e.tensor_tensor(out=STAT[:, 8:12], in0=STAT[:, 4:8], in1=NHALF[:, 0:4], op=ALU.pow),
              reads=[bf("STb"), cst], writes=[bf("STc")] + bl("STc", 4))
        for t in range(4):
            P.add("dve", lambda e, t=t: e.scalar_tensor_tensor(out=X[:, t, :], in0=X[:, t, :], scalar=STAT[:, 8 + t:9 + t], in1=FNG[:],
                                                               op0=ALU.mult, op1=ALU.mult), reads=[bf("X%d" % t), bf("STc"), cst], writes=[bf("X%d" % t)])
            P.add("sp", lambda e, t=t: e.dma_start(out=out[row0 + t * 128:row0 + (t + 1) * 128, :], in_=X[:, t, :]),
                  reads=[bf("X%d" % t)], writes=[bf("OUT%d" % t)], dma=True)
        if g + 1 < ng:
            load_X(g + 1)

    sweepB_ctx()
    ng = {None: NG, "B_2": 2}.get(stop_after, 1)
    for g in range(ng):
        sweepB_group(g, ng)
    P.emit()
    return nc, P


def make_in_maps(inp, cores=range(8)):
    f = lambda a: np.ascontiguousarray(a, dtype=np.float32)
    vecs = np.stack([inp["norm1_g"][0], inp["norm2_g"][0], inp["gla_norm_g"][0]], 0)
    vecs_fm = f(vecs.reshape(3, 8, 128).transpose(2, 0, 1))
    rows = f(np.stack([inp["final_norm_g"], inp["ln_v_g"][0], inp["ln_v_b"][0], inp["b_spatial"][0].reshape(-1)], 0))
    walpha = f(np.stack([np.concatenate([inp["w_alpha_f"][0], inp["b_alpha_f"]], 0),
                         np.concatenate([inp["w_alpha_b"][0], inp["b_alpha_b"]], 0)], 0))
    shared = dict(w_ada=f(inp["w_ada"][0]), b_ada=f(inp["b_ada"]), vecs_fm=vecs_fm, rows=rows, w_in=f(inp["w_in"][0]),
                  w_spatial=f(inp["w_spatial"][0]), walpha=walpha, w_branch_a=f(inp["w_branch_a"][0]),
                  w_branch_b=f(inp["w_branch_b"][0]), w_out=f(inp["w_out"][0]), w_ffn_in=f(inp["w_ffn_in"][0]),
                  w_ffn_out=f(inp["w_ffn_out"][0]))
    maps = []
    for b in cores:
        m = dict(shared)
        m["x"] = f(inp["x"][b])
        m["ctx"] = f(inp["ctx"][b])
        m["c2"] = f(np.stack([inp["c"][b], inp["c_ctx"]], 1))
        maps.append(m)
    return maps


_NC_CACHE = {}


def kernel(**inputs):
    inp = {k: np.asarray(v) for k, v in inputs.items()}
    if "nc" not in _NC_CACHE:
        _NC_CACHE["nc"] = build_nc()[0]
    nc = _NC_CACHE["nc"]
    maps = make_in_maps(inp)
    res = run_bass_kernel_spmd(nc, maps, core_ids=list(range(8)))
    return np.stack([np.asarray(r["out"], dtype=np.float32) for r in res.results], 0)
```
